# Optimizing a Trainium2 kernel written in Bass

```python
import jax, jax.numpy as jnp
from jax import lax
import numpy as np

D_MODEL = 1024
BATCH = 8
SEQ = 4096
DEPTH = 1

CTX_LEN = 256
GRID_W = 64

CONV_DIM = D_MODEL
CONV_K = 31
SSM_EXPAND = 2
SSM_INNER = SSM_EXPAND * D_MODEL
SSM_HEAD_DIM = 64
SSM_HEADS = SSM_INNER // SSM_HEAD_DIM
SSM_GROUPS = 8
SSM_HPG = SSM_HEADS // SSM_GROUPS
SSM_STATE = 128
SSM_CONV_K = 5
SSM_CHUNK = 128
SSM_GN = SSM_GROUPS * SSM_STATE
XBC_DIM = SSM_INNER + 2 * SSM_GN
D_FF = 4 * D_MODEL
N_BRANCH = 2
COL_XBC = 0
COL_DT = COL_XBC + XBC_DIM
COL_Z = COL_DT + 2 * SSM_HEADS
COL_GLU = COL_Z + SSM_INNER
COL_GATE = COL_GLU + 2 * CONV_DIM
PROJ_DIM = COL_GATE + N_BRANCH * D_MODEL
EPS = 1e-6

kernel_name = "hybrid_conformer_ssd_prefix_dit_block"

F32 = jnp.float32


def rmsnorm(x, w):
    xf = x.astype(F32)
    y = xf * lax.rsqrt(jnp.mean(xf * xf, axis=-1, keepdims=True) + EPS)
    return (y * w.astype(F32)).astype(x.dtype)


def layernorm(x, w, b):
    xf = x.astype(F32)
    mu = jnp.mean(xf, axis=-1, keepdims=True)
    xc = xf - mu
    var = jnp.mean(xc * xc, axis=-1, keepdims=True)
    return (xc * lax.rsqrt(var + EPS) * w.astype(F32) + b.astype(F32)).astype(x.dtype)


def adaln(cond, w, b):
    m = (jax.nn.silu(cond) @ w + b)[:, None, :]
    return jnp.split(m, 6, axis=-1)


def modulate(h, shift, scale):
    return h * (1 + scale) + shift


def dwconv(x, w, b):
    y = lax.conv_general_dilated(x, w[:, None, :].astype(x.dtype), window_strides=(1,),
                                 padding='SAME', dimension_numbers=('NWC', 'WIO', 'NWC'),
                                 feature_group_count=x.shape[-1])
    return y + b


def flip(t):
    return jnp.flip(t, axis=1)


def ssd_scan(xs, dt, a, bm, cm, h0):
    b, l, h, p = xs.shape
    g, n = bm.shape[2], bm.shape[3]
    r = h // g
    q = SSM_CHUNK
    c = l // q
    x5 = xs.astype(F32).reshape(b, c, q, g, r, p)
    dts = dt.astype(F32).reshape(b, c, q, g, r)
    bs = bm.astype(F32).reshape(b, c, q, g, n)
    cs_ = cm.astype(F32).reshape(b, c, q, g, n)
    cum = jnp.cumsum(dts * a.astype(F32).reshape(g, r), axis=2)
    xdt = x5 * dts[..., None]
    cum_t = jnp.moveaxis(cum, 2, -1)
    seg = cum_t[..., :, None] - cum_t[..., None, :]
    lower = jnp.tril(jnp.ones((q, q), dtype=bool))
    decay = jnp.exp(jnp.where(lower, seg, -jnp.inf))
    cb = jnp.einsum('bcqgn,bckgn->bcgqk', cs_, bs)
    y_diag = jnp.einsum('bcgrqk,bckgrp->bcqgrp', cb[:, :, :, None] * decay, xdt)
    decay_to_end = jnp.exp(cum[:, :, -1:] - cum)
    states = jnp.einsum('bcqgn,bcqgr,bcqgrp->bcgrpn', bs, decay_to_end, xdt)
    chunk_decay = jnp.exp(cum[:, :, -1])

    def step(hc, inp):
        s, d = inp
        return hc * d[..., None, None] + s, hc

    h_final, h_in = lax.scan(step, h0.astype(F32),
                             (jnp.moveaxis(states, 1, 0), jnp.moveaxis(chunk_decay, 1, 0)))
    h_in = jnp.moveaxis(h_in, 0, 1)
    y_off = jnp.einsum('bcqgn,bcgrpn,bcqgr->bcqgrp', cs_, h_in, jnp.exp(cum))
    return (y_diag + y_off).reshape(b, l, h, p), h_final


def ssd_final_state(xs, dt, a, bm):
    b, l, h, p = xs.shape
    g = bm.shape[2]
    r = h // g
    dtf = dt.astype(F32).reshape(b, l, g, r)
    cum = jnp.cumsum(dtf * a.astype(F32).reshape(g, r), axis=1)
    wgt = jnp.exp(cum[:, -1:] - cum) * dtf
    return jnp.einsum('blgn,blgr,blgrp->bgrpn', bm.astype(F32), wgt,
                      xs.astype(F32).reshape(b, l, g, r, p))


def ssd_inputs(proj, lp):
    b, l, _ = proj.shape
    xbc = jax.nn.silu(dwconv(proj[..., COL_XBC:COL_DT], lp['ssm_conv_w'], lp['ssm_conv_b']))
    xs = xbc[..., :SSM_INNER].reshape(b, l, SSM_HEADS, SSM_HEAD_DIM)
    bm = xbc[..., SSM_INNER:SSM_INNER + SSM_GN].reshape(b, l, SSM_GROUPS, SSM_STATE)
    cm = xbc[..., SSM_INNER + SSM_GN:].reshape(b, l, SSM_GROUPS, SSM_STATE)
    dt = jax.nn.softplus(proj[..., COL_DT:COL_Z].reshape(b, l, 2, SSM_HEADS).astype(F32)
                         + lp['ssm_dt_bias'].astype(F32))
    return xs, bm, cm, dt[:, :, 0], dt[:, :, 1]


def token_mixer(proj, n_seg, seg_len, h_f, h_b, a, lp):
    bsz, l, _ = proj.shape
    glu = proj[..., COL_GLU:COL_GATE]
    u = glu[..., :CONV_DIM] * jax.nn.sigmoid(glu[..., CONV_DIM:])
    u = dwconv(u.reshape(bsz * n_seg, seg_len, CONV_DIM), lp['conv_dw_w'], lp['conv_dw_b'])
    u = jax.nn.silu(layernorm(u.reshape(bsz, l, CONV_DIM), lp['conv_ln_w'], lp['conv_ln_b']))
    u_conv = u @ lp['w_conv_out'] + lp['b_conv_out']
    xs, bm, cm, dt_f, dt_b = ssd_inputs(proj, lp)
    y_f, hf_out = ssd_scan(xs, dt_f, a[0], bm, cm, h_f)
    y_b, hb_out = ssd_scan(flip(xs), flip(dt_b), a[1], flip(bm), flip(cm), h_b)
    y = y_f + flip(y_b) + lp['ssm_d'].astype(F32)[:, None] * xs.astype(F32)
    y = y.reshape(bsz, l, SSM_INNER).astype(proj.dtype)
    y = rmsnorm(y * jax.nn.silu(proj[..., COL_Z:COL_GLU]), lp['ssm_norm_w'])
    u_ssd = y @ lp['w_ssm_out']
    g_conv, g_ssd = jnp.split(jax.nn.sigmoid(proj[..., COL_GATE:]), 2, axis=-1)
    return (g_conv * u_conv + g_ssd * u_ssd) @ lp['w_o'], hf_out, hb_out


def sq_relu_mlp(h, w1, w2):
    return jnp.square(jax.nn.relu(h @ w1)) @ w2


def setup_inputs(seed: int = 0) -> dict:
    key = jax.random.key(seed)
    ks = jax.random.split(key, 32)
    nrm = jax.random.normal
    D, L = D_MODEL, DEPTH
    dt0 = jnp.exp(jax.random.uniform(ks[16], (L, 2, SSM_HEADS), minval=np.log(1e-3), maxval=np.log(1e-1)))
    return {
        'x': nrm(ks[0], (BATCH, SEQ, D), F32),
        'c': nrm(ks[1], (BATCH, D), F32),
        'ctx': nrm(ks[2], (BATCH, CTX_LEN, D), F32),
        'c_ctx': nrm(ks[3], (D,), F32),
        'w_ada': nrm(ks[4], (L, D, 6 * D), F32) * D ** -0.5,
        'b_ada': nrm(ks[5], (L, 6 * D), F32) * 0.02,
        'norm1_w': 1.0 + 0.02 * nrm(ks[6], (L, D), F32),
        'norm2_w': 1.0 + 0.02 * nrm(ks[7], (L, D), F32),
        'w_in': nrm(ks[8], (L, D, PROJ_DIM), F32) * D ** -0.5,
        'conv_dw_w': nrm(ks[9], (L, CONV_K, CONV_DIM), F32) * CONV_K ** -0.5,
        'conv_dw_b': nrm(ks[10], (L, CONV_DIM), F32) * 0.02,
        'conv_ln_w': 1.0 + 0.02 * nrm(ks[11], (L, CONV_DIM), F32),
        'conv_ln_b': nrm(ks[12], (L, CONV_DIM), F32) * 0.02,
        'w_conv_out': nrm(ks[13], (L, CONV_DIM, D), F32) * CONV_DIM ** -0.5,
        'b_conv_out': nrm(ks[14], (L, D), F32) * 0.02,
        'ssm_conv_w': nrm(ks[15], (L, SSM_CONV_K, XBC_DIM), F32) * SSM_CONV_K ** -0.5,
        'ssm_conv_b': nrm(ks[17], (L, XBC_DIM), F32) * 0.02,
        'ssm_dt_bias': (dt0 + jnp.log(-jnp.expm1(-dt0))).astype(F32),
        'ssm_a_log': jnp.log(jax.random.uniform(ks[18], (L, 2, SSM_HEADS), minval=1.0, maxval=16.0)).astype(F32),
        'ssm_d': 1.0 + 0.02 * nrm(ks[19], (L, SSM_HEADS), F32),
        'ssm_norm_w': 1.0 + 0.02 * nrm(ks[20], (L, SSM_INNER), F32),
        'w_ssm_out': nrm(ks[21], (L, SSM_INNER, D), F32) * SSM_INNER ** -0.5,
        'w_o': nrm(ks[22], (L, D, D), F32) * D ** -0.5,
        'w_mlp1': nrm(ks[23], (L, D, D_FF), F32) * D ** -0.5,
        'w_mlp2': nrm(ks[24], (L, D_FF, D), F32) * D_FF ** -0.5,
        'final_norm_w': 1.0 + 0.02 * nrm(ks[25], (D,), F32),
    }


def reference(x, c, ctx, c_ctx, w_ada, b_ada, norm1_w, norm2_w, w_in, conv_dw_w, conv_dw_b,
              conv_ln_w, conv_ln_b, w_conv_out, b_conv_out, ssm_conv_w, ssm_conv_b,
              ssm_dt_bias, ssm_a_log, ssm_d, ssm_norm_w, w_ssm_out, w_o, w_mlp1, w_mlp2,
              final_norm_w):
    bsz, seq, _ = x.shape
    rows = seq // GRID_W
    ctx_len = ctx.shape[1]
    for i in range(DEPTH):
        last = i == DEPTH - 1
        lp = {
            'conv_dw_w': conv_dw_w[i], 'conv_dw_b': conv_dw_b[i],
            'conv_ln_w': conv_ln_w[i], 'conv_ln_b': conv_ln_b[i],
            'w_conv_out': w_conv_out[i], 'b_conv_out': b_conv_out[i],
            'ssm_conv_w': ssm_conv_w[i], 'ssm_conv_b': ssm_conv_b[i],
            'ssm_dt_bias': ssm_dt_bias[i], 'ssm_d': ssm_d[i], 'ssm_norm_w': ssm_norm_w[i],
            'w_ssm_out': w_ssm_out[i], 'w_o': w_o[i],
        }
        a = -jnp.exp(ssm_a_log[i].astype(F32))
        mod = adaln(c, w_ada[i], b_ada[i])
        mod_c = adaln(c_ctx[None, :], w_ada[i], b_ada[i])
        hc = modulate(rmsnorm(ctx, norm1_w[i]), mod_c[0], mod_c[1])
        if last:
            pc = hc @ w_in[i][:, :COL_Z]
            xs_c, b_c, _, dtf_c, dtb_c = ssd_inputs(pc, lp)
            h_f = ssd_final_state(xs_c, dtf_c, a[0], b_c)
            h_b = ssd_final_state(flip(xs_c), flip(dtb_c), a[1], flip(b_c))
        else:
            zero_state = jnp.zeros((bsz, SSM_GROUPS, SSM_HPG, SSM_HEAD_DIM, SSM_STATE), F32)
            mix_c, h_f, h_b = token_mixer(hc @ w_in[i], 1, ctx_len, zero_state, zero_state, a, lp)
            ctx = ctx + mod_c[2] * mix_c
            hc2 = modulate(rmsnorm(ctx, norm2_w[i]), mod_c[3], mod_c[4])
            ctx = ctx + mod_c[5] * sq_relu_mlp(hc2, w_mlp1[i], w_mlp2[i])
        h = modulate(rmsnorm(x, norm1_w[i]), mod[0], mod[1])
        mix, _, _ = token_mixer(h @ w_in[i], rows, GRID_W, h_f, h_b, a, lp)
        x = x + mod[2] * mix
        h2 = modulate(rmsnorm(x, norm2_w[i]), mod[3], mod[4])
        x = x + mod[5] * sq_relu_mlp(h2, w_mlp1[i], w_mlp2[i])
    return rmsnorm(x, final_norm_w)
```

```python
import numpy as np
import concourse.bass as bass
import concourse.mybir as mybir
from concourse.bass_utils import run_bass_kernel_spmd
from contextlib import ExitStack

F32 = mybir.dt.float32
BF16 = mybir.dt.bfloat16
AF = mybir.ActivationFunctionType
ALU = mybir.AluOpType

D = 1024
NT = 4096
NCX = 256
PROJ = 10304
C_DT, C_Z, C_GLU, C_GATE = 4096, 4160, 6208, 8256
EPS = 1e-6
O_BADA, O_N1, O_N2, O_CDW, O_CDB, O_CLW, O_CLB, O_BCO, O_SCW, O_SCB, O_SNW, NV = 0, 48, 56, 64, 312, 320, 328, 336, 344, 504, 536, 552
O_BG1, O_BG5, NWA = 0, 1024, 2048
O_FNW, O_DTB, O_ALOG, O_SD, NW = 0, 1024, 1088, 1152, 1184
K_ID, K_U, K_L, K_SL, K_SU, K_ONE, K_NEGF, K_NEGB, NK = 0, 128, 256, 384, 512, 640, 768, 896, 1024


class Sem:
    __slots__ = ("h", "n", "dma")


class Buf:
    __slots__ = ("name", "w", "r", "dsem", "excl")


class Prog:
    def __init__(self, nc, st):
        self.nc = nc
        self.st = st
        self.k = 0
        self.eng = {"pe": nc.tensor, "act": nc.scalar, "dve": nc.vector, "pool": nc.gpsimd, "sp": nc.sync}
        self.esem = {e: self.new_sem(False) for e in ("pe", "act", "dve", "pool")}
        self.seen = {e: {} for e in self.eng}
        self.dpool = [self.new_sem(True) for _ in range(64)]
        self.di = 0
        self.allsems = []

    def new_sem(self, dma):
        s = Sem()
        s.h = self.st.enter_context(self.nc.semaphore(f"sm{self.k}"))
        self.k += 1
        s.n = 0
        s.dma = dma
        return s

    def buf(self, name):
        b = Buf()
        b.name = name
        b.w = None
        b.r = {}
        b.excl = name.startswith("ps") or name.startswith("pb")
        b.dsem = None
        return b

    def _dsem(self, b):
        if b.dsem is None:
            b.dsem = self.dpool[self.di % len(self.dpool)]
            self.di += 1
        return b.dsem

    def _waits(self, e, r, w):
        deps = {}

        def add(s, v):
            if deps.get(s, 0) < v:
                deps[s] = v
        own = self.esem.get(e)
        for b in r:
            if b.w:
                add(*b.w)
            if b.excl:
                for s, v in b.r.items():
                    if s is not own:
                        add(s, v)
        for b in w:
            if b.w:
                add(*b.w)
            for s, v in b.r.items():
                add(s, v)
        eng = self.eng[e]
        seen = self.seen[e]
        for s, v in deps.items():
            if seen.get(s, 0) >= v:
                continue
            if s.dma:
                v = s.n
            eng.wait_ge(s.h, v)
            seen[s] = v

    def _record(self, ev, r, w):
        s, v = ev
        for b in r:
            if b.r.get(s, 0) < v:
                b.r[s] = v
        for b in w:
            b.w = ev
            b.r = {}

    def op(self, e, fn, r=(), w=()):
        self._waits(e, r, w)
        ins = fn(self.eng[e])
        sem = self.esem[e]
        if sem.n >= 30000:
            sem = self.esem[e] = self.new_sem(False)
        sem.n += 1
        ins.then_inc(sem.h, 1)
        self._record((sem, sem.n), r, w)

    def dma(self, q, out, in_, r=(), w=(), sb=None):
        self._waits(q, r, w)
        ins = self.eng[q].dma_start(out=out, in_=in_)
        s = self._dsem(sb or (list(w) + list(r))[0])
        s.n += 16
        ins.then_inc(s.h, 16)
        self._record((s, s.n), r, w)

    def pe(self, groups, r, w):
        def fn(eng):
            ins = None
            for out_ap, pairs in groups:
                n = len(pairs)
                for i, (l, rr) in enumerate(pairs):
                    ins = eng.matmul(out_ap, l, rr, start=(i == 0), stop=(i == n - 1))
            return ins
        self.op("pe", fn, r=r, w=w)

    def pe_raw(self, mms, r, w):
        def fn(eng):
            ins = None
            for out_ap, l, rr, st_, sp_ in mms:
                ins = eng.matmul(out_ap, l, rr, start=st_, stop=sp_, skip_group_check=True)
            return ins
        self.op("pe", fn, r=r, w=w)

    def tr(self, items, ident, r, w):
        def fn(eng):
            ins = None
            for out_ap, in_ap in items:
                ins = eng.transpose(out_ap, in_ap, ident)
            return ins
        self.op("pe", fn, r=r, w=w)

    def barrier(self):
        sems = [s for s in self.esem.values()] + self.dpool
        for e in self.eng:
            eng = self.eng[e]
            seen = self.seen[e]
            for s in sems:
                if s.n > 0 and seen.get(s, 0) < s.n:
                    eng.wait_ge(s.h, s.n)
                    seen[s] = s.n


def build(debug=False, upto=99):
    nc = bass.Bass("TRN2", target_bir_lowering=False)

    def din(name, shape):
        return nc.dram_tensor(name, shape, F32, kind="ExternalInput").ap()

    skind = "ExternalOutput" if debug else "Internal"

    def dscr(name, shape, dt):
        return nc.dram_tensor(name, shape, dt, kind=skind).ap()

    x_d = din("x", [NT, D]); ctx_d = din("ctx", [NCX, D]); cT_d = din("cT", [128, 16])
    vfm_d = din("vfm", [128, NV]); vtm_d = din("vtm", [128, NW]); vtma_d = din("vtma", [128, NWA]); consts_d = din("consts", [128, NK]); sel_d = din("sel", [128, 4096])
    w_ada_d = din("w_ada", [D, 6 * D]); w_in_d = din("w_in", [D, PROJ]); w_co_d = din("w_conv_out", [D, D])
    w_so_d = din("w_ssm_out", [2 * D, D]); w_o_d = din("w_o", [D, D]); w_m1_d = din("w_mlp1", [D, 4 * D]); w_m2_d = din("w_mlp2", [4 * D, D])
    out_d = nc.dram_tensor("out", [NT, D], F32, kind="ExternalOutput").ap()

    HT_d = dscr("HT", [128, 8, NT + 4], BF16)
    XTM_d = dscr("XTM", [NT, 2048], BF16); BTM_d = dscr("BTM", [NT, 1024], BF16)
    BTF_d = dscr("BTF", [1024, NT], BF16); CTF_d = dscr("CTF", [1024, NT], BF16)
    XTMc_d = dscr("XTMc", [NCX, 2048], BF16); BTMc_d = dscr("BTMc", [NCX, 1024], BF16)
    YF_d = dscr("YF", [NT, 2048], F32); YB_d = dscr("YB", [NT, 2048], F32); YZT_d = dscr("YZT", [2048, NT], BF16)
    UC_d = dscr("UC", [1024, NT], BF16); X1_d = dscr("X1", [NT, D], F32); H2T_d = dscr("H2T", [128, 8, NT], BF16)
    DGC_d = dscr("DGC", [8, 128, 31 * 128], BF16)
    DBG_d = nc.dram_tensor("DBG", [128, 8192], F32, kind="ExternalOutput").ap() if debug else None

    w_in_r = w_in_d.rearrange("(k p) c -> p k c", p=128)

    with ExitStack() as st:
        E = st.enter_context
        P = Prog(nc, st)

        def T(stack, name, shape, dt):
            t = stack.enter_context(nc.sbuf_tensor("sb_" + name, shape, dt))
            return t, P.buf(name)

        PS = []
        for i in range(6):
            PS.append((E(nc.psum_tensor(f"ps{i}", [128, 512], F32)), P.buf(f"ps{i}")))
        PB = []
        for i in range(2):
            PB.append((E(nc.psum_tensor(f"pb{i}", [128, 1024], BF16)), P.buf(f"pb{i}")))

        c32, c32b = T(st, "c32", [128, NK], F32)
        cbf, cbfb = T(st, "cbf", [128, NK], BF16)
        vfm, vfmb = T(st, "vfm", [128, NV], F32)
        vtm, vtmb = T(st, "vtm", [128, NW], F32)
        cT, cTb = T(st, "cT", [128, 8, 2], F32)
        sc, scb_ = T(st, "sc", [128, 8, 2], F32)
        modfm, modfmb = T(st, "modfm", [128, 32, 2], F32)
        g1bc, g1b = T(st, "g1bc", [128, 1024], F32)
        g5bc, g5b = T(st, "g5bc", [128, 1024], F32)
        A1, A1b = T(st, "A1", [128, 8, 2], F32)
        A2, A2b = T(st, "A2", [128, 8, 1], F32)
        abc, abcb = T(st, "abc", [128, 64], F32)
        gssd = ExitStack()
        dtl, dtlb = T(gssd, "dtl", [128, 32, 64], F32)
        dtc, dtcb = T(gssd, "dtc", [128, 2, 64], F32)
        hst = [(T(gssd, f"hst{i}", [128, 2048], F32)[0], [P.buf(f"hst{i}_{g}") for g in range(8)]) for i in range(2)]
        hbf = [(T(gssd, f"hbf{i}", [128, 2048], BF16)[0], [P.buf(f"hbf{i}_{g}") for g in range(8)]) for i in range(2)]
        SEL, SELb = T(gssd, "SEL", [128, 32, 128], BF16)
        P.dma("pool", SEL[:], sel_d.rearrange("p (a b) -> p a b", b=128), w=[SELb])

        P.dma("sp", c32[:], consts_d, w=[c32b])
        P.dma("pool", cbf[:], consts_d, w=[cbfb])
        P.dma("sp", vfm[:], vfm_d, w=[vfmb])
        P.dma("sp", vtm[:], vtm_d, w=[vtmb])
        P.dma("sp", cT[:], cT_d.rearrange("p (k t) -> p k t", t=2), w=[cTb])

        id32 = c32[:, K_ID:K_ID + 128]; idbf = cbf[:, K_ID:K_ID + 128]
        U32 = c32[:, K_U:K_U + 128]; L32 = c32[:, K_L:K_L + 128]
        SL32 = c32[:, K_SL:K_SL + 128]; SU32 = c32[:, K_SU:K_SU + 128]
        one32 = c32[:, K_ONE:K_ONE + 128]; onebf = cbf[:, K_ONE:K_ONE + 128]
        Ubf = cbf[:, K_U:K_U + 128]; Lbf = cbf[:, K_L:K_L + 128]

        with ExitStack() as s0:
            slabs = [T(s0, f"adaslab{i}", [128, 8, 512], F32) for i in range(2)]
            scbc, scbcb = T(s0, "scbc", [128, 8, 128], F32)
            vtma, vtmab = T(s0, "vtma", [128, NWA], F32)
            P.dma("sp", vtma[:], vtma_d, w=[vtmab])
            P.op("act", lambda e: e.activation(out=sc[:], in_=cT[:], func=AF.Silu), r=[cTb], w=[scb_])
            P.op("dve", lambda e: e.tensor_copy(out=scbc[:], in_=sc[:, :, 0].unsqueeze(2).broadcast_to([128, 8, 128])), r=[scb_], w=[scbcb])
            w_ada_r = w_ada_d.rearrange("(k p) c -> p k c", p=128)
            fmbase = {0: 0, 1: 8, 3: 16, 4: 24}
            for s in range(12):
                sl, slb = slabs[s % 2]
                P.dma("sp", sl[:], w_ada_r[:, :, 512 * s:512 * s + 512], w=[slb])
                m, half = s // 2, s % 2
                ps, psb = PS[s % 2]
                if m in (2, 5):
                    P.pe([(ps[:], [(scbc[:, k, :], sl[:, k, :]) for k in range(8)])], r=[scbcb, slb], w=[psb])
                    gt, gtb, ob = (g1bc, g1b, O_BG1) if m == 2 else (g5bc, g5b, O_BG5)
                    P.op("dve", lambda e, gt=gt, ps=ps, ob=ob, half=half: e.tensor_tensor(
                        out=gt[:, 512 * half:512 * half + 512], in0=ps[:], in1=vtma[:, ob + 512 * half:ob + 512 * half + 512], op=ALU.add),
                        r=[psb, vtmab], w=[gtb])
                else:
                    P.pe([(ps[:, 2 * j:2 * j + 2], [(sl[:, k, 128 * j:128 * j + 128], sc[:, k, :]) for k in range(8)]) for j in range(4)],
                         r=[scb_, slb], w=[psb])
                    for j in range(4):
                        ci = fmbase[m] + half * 4 + j
                        col = O_BADA + m * 8 + half * 4 + j
                        P.op("dve", lambda e, ps=ps, j=j, ci=ci, col=col: e.tensor_scalar_add(
                            out=modfm[:, ci, :], in0=ps[:, 2 * j:2 * j + 2], scalar1=vfm[:, col:col + 1]), r=[psb, vfmb], w=[modfmb])
            P.op("dve", lambda e: e.scalar_tensor_tensor(out=A1[:], in0=modfm[:, 8:16, :], scalar=1.0,
                 in1=vfm[:, O_N1:O_N1 + 8].unsqueeze(2).broadcast_to([128, 8, 2]), op0=ALU.add, op1=ALU.mult), r=[modfmb, vfmb], w=[A1b])
            P.op("dve", lambda e: e.scalar_tensor_tensor(out=A2[:], in0=modfm[:, 24:32, 0:1], scalar=1.0,
                 in1=vfm[:, O_N2:O_N2 + 8].unsqueeze(2), op0=ALU.add, op1=ALU.mult), r=[modfmb, vfmb], w=[A2b])
            P.op("act", lambda e: e.activation(out=abc[:], in_=vtm[:, O_ALOG:O_ALOG + 64], func=AF.Exp), r=[vtmb], w=[abcb])
            P.op("dve", lambda e: e.tensor_scalar_mul(out=abc[:], in0=abc[:], scalar1=-1.0), r=[abcb], w=[abcb])
            for i in range(2):
                P.op("dve", lambda e, i=i: e.memset(hst[i][0][:], 0.0), w=hst[i][1])
                P.op("dve", lambda e, i=i: e.memset(hbf[i][0][:], 0.0), w=hbf[i][1])
            P.barrier()
        if debug:
            P.dma("sp", DBG_d[:, 0:64], modfm[:].rearrange("p a b -> p (a b)"), r=[modfmb])
            P.dma("sp", DBG_d[:, 64:64 + 1024], g1bc[:], r=[g1b])
            P.dma("sp", DBG_d[:, 1088:1088 + 1024], g5bc[:], r=[g5b])

        def rstd_of(ss_ap, ssb, n_feat):
            P.op("act", lambda e: e.activation(out=ss_ap, in_=ss_ap, func=AF.Sqrt, bias=EPS, scale=1.0 / n_feat), r=[ssb], w=[ssb])
            P.op("dve", lambda e: e.reciprocal(out=ss_ap, in_=ss_ap), r=[ssb], w=[ssb])

        def phase0(tag, src_d, n, Aap, Shap, Abufs, hT, hTb, stk):
            xr = [T(stk, f"{tag}x{i}", [128, D], F32) for i in range(3)]
            xn = [T(stk, f"{tag}xn{i}", [128, D], BF16) for i in range(2)]
            junk, junkb = T(stk, f"{tag}junk", [128, D], BF16)
            ss, ssb = T(stk, f"{tag}ss", [128, 32], F32)
            ssbs = [P.buf(f"{tag}ss{t}") for t in range(n // 128)]
            P.op("dve", lambda e: e.memset(ss[:], 0.0), w=ssbs)
            P.op("dve", lambda e: e.memset(hT[:, :, 0:2], 0.0), w=[hTb])
            P.op("dve", lambda e: e.memset(hT[:, :, n + 2:n + 4], 0.0), w=[hTb])
            def phL(t):
                x_, xb = xr[t % 3]
                P.dma("sp" if t % 2 == 0 else "act", x_[:], src_d[128 * t:128 * t + 128, :], w=[xb])

            def phA(t):
                x_, xb = xr[t % 3]
                sst = ss[:, t:t + 1]
                P.op("act", lambda e: e.activation(out=junk[:], in_=x_[:], func=AF.Square, accum_out=sst), r=[xb], w=[junkb, ssbs[t]])
                rstd_of(sst, ssbs[t], D)

            def phB(t):
                x_, xb = xr[t % 3]; xn_, xnb = xn[t % 2]
                sst = ss[:, t:t + 1]
                P.op("act", lambda e: e.activation(out=xn_[:], in_=x_[:], func=AF.Copy, scale=sst), r=[xb, ssbs[t]], w=[xnb])
                pb, pbb = PB[t % 2]
                P.tr([(pb[:, 128 * k:128 * k + 128], xn_[:, 128 * k:128 * k + 128]) for k in range(8)], idbf, r=[xnb, cbfb], w=[pbb])
                for k in range(8):
                    P.op("dve", lambda e, k=k: e.tensor_scalar(out=hT[:, k, 2 + 128 * t:2 + 128 * t + 128], in0=pb[:, 128 * k:128 * k + 128],
                         scalar1=Aap(k), scalar2=Shap(k), op0=ALU.mult, op1=ALU.add), r=[pbb] + Abufs, w=[hTb])

            ncht = n // 128
            phL(0)
            if ncht > 1:
                phL(1)
            phA(0)
            for t in range(ncht):
                if t + 2 < ncht:
                    phL(t + 2)
                if t + 1 < ncht:
                    phA(t + 1)
                phB(t)

        def xbc_stage(tag, n, hT, hTb, stk, XTM, BTM, BTF, CTF, dt_all, dt_allb, nslab, dbufs=None):
            TS = min(512, n); ntile = n // TS; nch = n // 128
            wsl = [T(stk, f"{tag}wsl{i}", [128, 8, 512], BF16) for i in range(2)]
            wdt, wdtb = T(stk, f"{tag}wdt", [128, 8, 64], BF16)
            dg = [T(stk, f"{tag}dg{i}", [128, 5, 128], BF16) for i in range(2)]
            raw = [T(stk, f"{tag}raw{i}", [128, n + 4], BF16) for i in range(2)]
            actb = [T(stk, f"{tag}act{i}", [128, n], BF16) for i in range(2)]
            xs = [T(stk, f"{tag}xs{i}", [128, nch, 128], BF16) for i in range(2)]
            dtmp, dtmpb = T(stk, f"{tag}dtmp", [128, 8, 64], F32)
            for i in range(2):
                P.op("dve", lambda e, i=i: e.memset(raw[i][0][:], 0.0), w=[raw[i][1]])
            P.dma("pool", wdt[:], w_in_r[:, :, C_DT:C_DT + 64], w=[wdtb])
            for t0 in range(0, nch, 8):
                m = min(8, nch - t0)
                ps, psb = PS[4]
                P.pe([(ps[:, 64 * j:64 * j + 64], [(hT[:, k, 2 + 128 * (t0 + j):2 + 128 * (t0 + j) + 128], wdt[:, k, :]) for k in range(8)])
                      for j in range(m)], r=[hTb, wdtb], w=[psb])
                P.op("dve", lambda e, m=m, ps=ps: e.tensor_tensor(out=dtmp[:, 0:m, :], in0=ps[:, 0:64 * m].rearrange("p (a b) -> p a b", b=64),
                     in1=vtm[:, O_DTB:O_DTB + 64].unsqueeze(1).broadcast_to([128, m, 64]), op=ALU.add), r=[psb, vtmb], w=[dtmpb])
                P.op("act", lambda e, m=m: e.activation(out=dtmp[:, 0:m, :], in_=dtmp[:, 0:m, :], func=AF.Exp), r=[dtmpb], w=[dtmpb])
                P.op("act", lambda e, m=m, t0=t0: e.activation(out=dt_all[:, t0:t0 + m, :], in_=dtmp[:, 0:m, :], func=AF.Ln, bias=1.0), r=[dtmpb], w=[dt_allb])
            nck = 4 * nslab
            slabs = {}

            def st1(c):
                sidx, cc = c // 4, c % 4
                if cc == 0:
                    ws, wsb = wsl[sidx % 2]
                    P.dma("pool", ws[:], w_in_r[:, :, 512 * sidx:512 * sidx + 512], w=[wsb])
                    slabs[sidx] = (ws, wsb)
                ws, wsb = slabs[sidx]
                dg_, dgb = dg[c % 2]; rw, rwb = raw[c % 2]
                for j in range(5):
                    col = O_SCW + c * 5 + j
                    P.op("dve", lambda e, j=j, col=col: e.tensor_scalar_mul(out=dg_[:, j, :], in0=id32, scalar1=vfm[:, col:col + 1]),
                         r=[c32b, vfmb], w=[dgb])
                for i in range(ntile):
                    bk = (i + c) % 2
                    ps, psb = PS[bk]
                    P.pe([(ps[:, 0:TS], [(ws[:, k, 128 * cc:128 * cc + 128], hT[:, k, 2 + TS * i:2 + TS * i + TS]) for k in range(8)])],
                         r=[wsb, hTb], w=[psb])
                    if bk == 0:
                        P.op("act", lambda e, ps=ps, i=i: e.activation(out=rw[:, 2 + TS * i:2 + TS * i + TS], in_=ps[:, 0:TS], func=AF.Copy), r=[psb], w=[rwb])
                    else:
                        P.op("dve", lambda e, ps=ps, i=i: e.tensor_copy(out=rw[:, 2 + TS * i:2 + TS * i + TS], in_=ps[:, 0:TS]), r=[psb], w=[rwb])

            def st2(c):
                dg_, dgb = dg[c % 2]; rw, rwb = raw[c % 2]; ab, abb = actb[c % 2]
                for i in range(ntile):
                    ps, psb = PS[2 + (i + c) % 2]
                    P.pe([(ps[:, 0:TS], [(dg_[:, j, :], rw[:, TS * i + j:TS * i + j + TS]) for j in range(5)])], r=[dgb, rwb], w=[psb])
                    P.op("act", lambda e, ps=ps, i=i: e.activation(out=ab[:, TS * i:TS * i + TS], in_=ps[:, 0:TS], func=AF.Silu,
                         bias=vfm[:, O_SCB + c:O_SCB + c + 1]), r=[psb, vfmb], w=[abb])

            def st3(c):
                ab, abb = actb[c % 2]; xs_, xsb = xs[c % 2]
                if c < 24:
                    for t0 in range(0, nch, 8):
                        m = min(8, nch - t0)
                        pb, pbb = PB[(t0 // 8) % 2]
                        P.tr([(pb[:, 128 * j:128 * j + 128], ab[:, 128 * (t0 + j):128 * (t0 + j) + 128]) for j in range(m)], idbf, r=[abb, cbfb], w=[pbb])
                        P.op("dve", lambda e, pb=pb, m=m, t0=t0: e.tensor_copy(out=xs_[:, t0:t0 + m, :],
                             in_=pb[:, 0:128 * m].rearrange("p (a b) -> p a b", b=128)), r=[pbb], w=[xsb])
                    dst, cc0 = (XTM, c) if c < 16 else (BTM, c - 16)
                    g = c - 16
                    for t0 in range(0, nch, 8):
                        m = min(8, nch - t0)
                        P.dma("sp", dst.rearrange("(t p) c -> p t c", p=128)[:, t0:t0 + m, 128 * cc0:128 * cc0 + 128], xs_[:, t0:t0 + m, :], r=[xsb],
                              w=([] if dbufs is None else [dbufs[0 if c < 16 else 1]]), sb=xsb)
                    if c >= 16:
                        if BTF is not None:
                            P.dma("sp", BTF[128 * g:128 * g + 128, :], ab[:], r=[abb])
                else:
                    g = c - 24
                    P.dma("sp", CTF[128 * g:128 * g + 128, :], ab[:], r=[abb])

            for step in range(nck + 2):
                if step < nck:
                    st1(step)
                if 0 <= step - 1 < nck:
                    st2(step - 1)
                if 0 <= step - 2 < nck:
                    st3(step - 2)

        def ssd_pass(tag, n, dirn, mode, XTM, BTM, BTF, CTF, dt_all, dt_allb, stk, Y_d=None, dbufs=None):
            nch = n // 128; TS = min(512, n); cpt = TS // 128
            order = list(range(nch)) if dirn == 0 else list(range(nch - 1, -1, -1))
            Mqbf = Ubf if dirn == 0 else Lbf
            Msbf = cbf[:, K_SL:K_SL + 128] if dirn == 0 else cbf[:, K_SU:K_SU + 128]
            Mcbf = Ubf if dirn == 0 else Lbf
            Mkbf = Ubf if dirn == 0 else Lbf
            hs, hsbs = hst[dirn]; hb, hbbs = hbf[dirn]
            full = mode != "state"
            Xr = [T(stk, f"{tag}X{i}", [128, 32, 64], BF16) for i in range(2)]
            Br = [T(stk, f"{tag}B{i}", [128, 1024], BF16) for i in range(2)]
            adtr = [T(stk, f"{tag}adt{i}", [128, 32], F32) for i in range(2)]
            adhl = [T(stk, f"{tag}adhl{i}", [128, 2, 32], BF16) for i in range(2)]
            exr = [T(stk, f"{tag}ex{i}", [128, 96], F32) for i in range(2)]
            wgtr = [T(stk, f"{tag}wgt{i}", [128, 32], F32) for i in range(2)]
            xdtwr = [T(stk, f"{tag}xdtw{i}", [128, 32, 64], BF16) for i in range(2)]
            if full:
                BTr = [T(stk, f"{tag}BT{i}", [128, 8, TS], BF16) for i in range(2)]
                CTr = [T(stk, f"{tag}CT{i}", [128, 8, TS], BF16) for i in range(2)]
                ah3r = [T(stk, f"{tag}ah3{i}", [128, 2, 96], BF16) for i in range(2)]
                c3r = [T(stk, f"{tag}c3{i}", [128, 128], BF16) for i in range(2)]
                nc3r = [T(stk, f"{tag}nc3{i}", [128, 128], BF16) for i in range(2)]
                R1, R1b = T(stk, f"{tag}R1", [128, 128], F32)
                tM, tMb = T(stk, f"{tag}tM", [128, 128], BF16)
                NEGbf = cbf[:, K_NEGF:K_NEGF + 128] if dirn == 0 else cbf[:, K_NEGB:K_NEGB + 128]
                xdtr = [T(stk, f"{tag}xdt{i}", [128, 32, 64], BF16) for i in range(2)]
                LT = [T(stk, f"{tag}LT{i}", [128, 4, 128], BF16) for i in range(2)]
                MT = [T(stk, f"{tag}MT{i}", [128, 4, 128], BF16) for i in range(2)]
                t1 = [T(stk, f"{tag}t1{i}", [128, 4, 64], BF16) for i in range(3)]
                ychr = [T(stk, f"{tag}ych{i}", [128, 2048], F32) for i in range(2)]
            state = {"cur_tile": None, "ntl": 0, "tiles": {}}
            ctx_ = {}

            def prologue(idx):
                t = order[idx]
                i = t // cpt
                c = {}
                if full:
                    if i != state["cur_tile"]:
                        state["cur_tile"] = i
                        BT, BTb = BTr[state["ntl"] % 2]; CT, CTb = CTr[state["ntl"] % 2]
                        state["ntl"] += 1
                        P.dma("act", BT[:], BTF.rearrange("(g n) t -> n g t", n=128)[:, :, TS * i:TS * i + TS], w=[BTb])
                        P.dma("act", CT[:], CTF.rearrange("(g n) t -> n g t", n=128)[:, :, TS * i:TS * i + TS], w=[CTb])
                        state["tiles"][i] = (BT, BTb, CT, CTb)
                    c["BT"], c["BTb"], c["CT"], c["CTb"] = state["tiles"][i]
                X, Xb = Xr[idx % 2]; Bt, Btb = Br[idx % 2]
                P.dma("sp", X[:], XTM[128 * t:128 * t + 128, :].rearrange("p (h d) -> p h d", d=64), w=[Xb], r=([] if dbufs is None else [dbufs[0]]))
                P.dma("sp", Bt[:], BTM[128 * t:128 * t + 128, :], w=[Btb], r=([] if dbufs is None else [dbufs[1]]))
                adt, adtb = adtr[idx % 2]; ah, ahb = adhl[idx % 2]; ex, exb = exr[idx % 2]; wgt, wgtb = wgtr[idx % 2]
                xdtw, xdtwb = xdtwr[idx % 2]
                dtcur = dt_all[:, t, 32 * dirn:32 * dirn + 32]
                P.op("dve", lambda e: e.tensor_tensor(out=adt[:], in0=dtcur, in1=abc[:, 32 * dirn:32 * dirn + 32], op=ALU.mult), r=[dt_allb, abcb], w=[adtb])
                P.op("dve", lambda e: e.tensor_copy(out=ah[:, 0, :], in_=adt[:]), r=[adtb], w=[ahb])
                P.op("dve", lambda e: e.tensor_tensor(out=ah[:, 1, :], in0=adt[:], in1=ah[:, 0, :], op=ALU.subtract), r=[adtb, ahb], w=[ahb])
                ps0, ps0b = PS[0]
                grp0 = [(ps0[:, 0:32], [(Mcbf, ah[:, 0, :]), (Mcbf, ah[:, 1, :])]), (ps0[:, 32:64], [(Msbf, ah[:, 0, :]), (Msbf, ah[:, 1, :])]),
                        (ps0[:, 64:96], [(onebf, ah[:, 0, :]), (onebf, ah[:, 1, :])])]
                rd0 = [cbfb, ahb]
                if full:
                    ah3, ah3b = ah3r[idx % 2]
                    P.op("dve", lambda e: e.tensor_copy(out=ah3[:].rearrange("p s (i r) -> p s i r", i=3), in_=ah[:].unsqueeze(2).broadcast_to([128, 2, 3, 32])),
                         r=[ahb], w=[ah3b])
                    grp0.append((ps0[0:96, 128:256], [(ah3[:, 0, :], Mcbf), (ah3[:, 1, :], Mcbf)]))
                    rd0.append(ah3b)
                P.pe(grp0, r=rd0, w=[ps0b])
                if full:
                    c3, c3b = c3r[idx % 2]; nc3, nc3b = nc3r[idx % 2]
                    cps = ps0[0:96, 128:256]
                    P.op("dve", lambda e: e.tensor_copy(out=c3[0:96, :], in_=cps), r=[ps0b], w=[c3b])
                    P.op("dve", lambda e: e.tensor_tensor(out=R1[0:96, :], in0=cps, in1=c3[0:96, :], op=ALU.subtract), r=[ps0b, c3b], w=[R1b])
                    P.op("dve", lambda e: e.tensor_copy(out=c3[32:64, :], in_=R1[32:64, :]), r=[R1b], w=[c3b])
                    P.op("dve", lambda e: e.tensor_copy(out=tM[64:96, :], in_=R1[64:96, :]), r=[R1b], w=[tMb])
                    P.op("dve", lambda e: e.tensor_tensor(out=c3[64:96, :], in0=R1[64:96, :], in1=tM[64:96, :], op=ALU.subtract), r=[R1b, tMb], w=[c3b])
                    P.op("dve", lambda e: e.tensor_scalar_mul(out=nc3[0:96, :], in0=c3[0:96, :], scalar1=-1.0), r=[c3b], w=[nc3b])
                P.op("act", lambda e: e.activation(out=ex[:], in_=ps0[:, 0:96], func=AF.Exp), r=[ps0b], w=[exb])
                P.op("dve", lambda e: e.tensor_tensor(out=wgt[:], in0=ex[:, 32:64], in1=dtcur, op=ALU.mult), r=[exb, dt_allb], w=[wgtb])
                P.op("pool" if full else "dve", lambda e: e.tensor_tensor(out=xdtw[:], in0=X[:], in1=wgt[:].unsqueeze(2).broadcast_to([128, 32, 64]), op=ALU.mult), r=[Xb, wgtb], w=[xdtwb])
                c.update(t=t, off=(t % cpt) * 128, X=X, Xb=Xb, Bt=Bt, Btb=Btb, ex=ex, exb=exb, xdtw=xdtw, xdtwb=xdtwb)
                if full:
                    xdt, xdtb = xdtr[idx % 2]
                    c.update(c3=c3, c3b=c3b, nc3=nc3, nc3b=nc3b)
                    P.op("pool", lambda e: e.tensor_tensor(out=xdt[:], in0=X[:], in1=dtcur.unsqueeze(2).broadcast_to([128, 32, 64]), op=ALU.mult), r=[Xb, dt_allb], w=[xdtb])
                    c.update(xdt=xdt, xdtb=xdtb)
                ctx_[idx] = c

            def decay_state(idx):
                ex, exb = ctx_[idx]["ex"], ctx_[idx]["exb"]
                P.op("pool" if full else "dve", lambda e: e.tensor_tensor(out=hs[:].rearrange("p (a b) -> p a b", b=64), in0=hs[:].rearrange("p (a b) -> p a b", b=64),
                     in1=ex[:, 64:96].unsqueeze(2).broadcast_to([128, 32, 64]), op=ALU.mult), r=[exb] + hsbs, w=hsbs)

            cbanks = [PS[1], (PB[1][0][:].bitcast(F32), PB[1][1])]

            def pre(c, g):
                BT, BTb, CT, CTb, off = c["BT"], c["BTb"], c["CT"], c["CTb"], c["off"]
                c3, c3b, nc3, nc3b = c["c3"], c["c3b"], c["nc3"], c["nc3b"]
                cbk, cbkb = cbanks[g % 2]
                P.pe([(cbk[:, 0:128], [(BT[:, g, off:off + 128], CT[:, g, off:off + 128])])], r=[BTb, CTb], w=[cbkb])
                sg, sgb = PS[2 + g % 2]
                mms = [(sg[:], idbf, NEGbf.unsqueeze(1).broadcast_to([128, 4, 128]), True, False)]
                for r in range(4):
                    mms.append((sg[:, 128 * r:128 * r + 128], SEL[0:96, 4 * g + r, :], c3[0:96, :], False, False))
                mms.append((sg[:], nc3[0:96, :], SEL[0:96, 4 * g:4 * g + 4, :].rearrange("p a b -> p (a b)"), False, True))
                P.pe_raw(mms, r=[cbfb, SELb, c3b, nc3b], w=[sgb])

            def mid(g):
                cbk, cbkb = cbanks[g % 2]
                sg, sgb = PS[2 + g % 2]
                lt, ltb = LT[g % 2]; mt, mtb = MT[g % 2]
                P.op("act", lambda e: e.activation(out=lt[:].rearrange("p a b -> p (a b)"), in_=sg[:], func=AF.Exp), r=[sgb], w=[ltb])
                P.op("dve", lambda e: e.tensor_tensor(out=mt[:], in0=lt[:], in1=cbk[:, 0:128].unsqueeze(1).broadcast_to([128, 4, 128]), op=ALU.mult), r=[ltb, cbkb], w=[mtb])

            def groups(idx):
                c = ctx_.pop(idx)
                t, off, ex, exb = c["t"], c["off"], c["ex"], c["exb"]
                Bt, Btb, xdtw, xdtwb = c["Bt"], c["Btb"], c["xdtw"], c["xdtwb"]
                Sb = PS[0][1]
                Sfull = PS[0][0]
                ybanks = [PS[4], PS[5], (PB[0][0][:].bitcast(F32), PB[0][1])]
                if full:
                    CT, CTb = c["CT"], c["CTb"]
                    xdt, xdtb = c["xdt"], c["xdtb"]
                    ych, ychb = ychr[idx % 2]
                    if idx == 0:
                        pre(c, 0)
                        mid(0)
                def fin(g):
                    Yf, Yfb = ybanks[g % 3]; tf, tfb = t1[g % 3]
                    P.pe_raw([(Yf[:, 0:256], idbf, tf[:].rearrange("p a b -> p (a b)"), False, True)], r=[cbfb, tfb], w=[Yfb])
                    P.op("act", lambda e: e.activation(out=ych[:, 256 * g:256 * g + 256], in_=Yf[:, 0:256], func=AF.Copy), r=[Yfb], w=[ychb])

                def hbcopy(g):
                    P.op("act", lambda e: e.activation(out=hb[:, 256 * g:256 * g + 256], in_=hs[:, 256 * g:256 * g + 256], func=AF.Copy), r=[hsbs[g]], w=[hbbs[g]])
                for g in range(8):
                    so = 256
                    if full:
                        mt, mtb = MT[g % 2]
                        if g < 7:
                            pre(c, g + 1)
                            mid(g + 1)
                        elif idx + 1 < nch:
                            pre(ctx_[idx + 1], 0)
                            mid(0)
                        Y, Yb = ybanks[g % 3]
                        grp = [(Y[:, 64 * r:64 * r + 64], mt[:, r, :], xdt[:, 4 * g + r, :], r == 0, False) for r in range(4)]
                        grp.append((Y[:, 256:512], CT[:, g, off:off + 128], hb[:, 256 * g:256 * g + 256], False, False))
                        P.pe_raw(grp, r=[mtb, xdtb, CTb, hbbs[g]], w=[Yb])
                        if g >= 1:
                            fin(g - 1)
                    if g >= 1:
                        hbcopy(g - 1)
                    P.pe([(Sfull[:, so:so + 256], [(Bt[:, 128 * g:128 * g + 128], xdtw[:, 4 * g:4 * g + 4, :].rearrange("p a b -> p (a b)"))])], r=[Btb, xdtwb], w=[Sb])
                    if full:
                        t1_, t1b = t1[g % 3]
                        P.op("dve", lambda e: e.tensor_tensor(out=t1_[:], in0=Y[:, 256:512].rearrange("p (a b) -> p a b", b=64),
                             in1=ex[:, 4 * g:4 * g + 4].unsqueeze(2).broadcast_to([128, 4, 64]), op=ALU.mult), r=[Yb, exb], w=[t1b])
                    hv = hs[:, 256 * g:256 * g + 256]
                    P.op("dve", lambda e: e.tensor_tensor(out=hv, in0=hv, in1=Sfull[:, 256:512], op=ALU.add), r=[Sb, hsbs[g]], w=[hsbs[g]])
                hbcopy(7)
                if full:
                    fin(7)
                    P.dma("sp", Y_d[128 * t:128 * t + 128, :], ych[:], r=[ychb])

            prologue(0)
            for idx in range(nch):
                decay_state(idx)
                if idx + 1 < nch:
                    prologue(idx + 1)
                groups(idx)

        def ssd_epilogue(stk):
            n = NT; nch = n // 128; TS = 512; cpt = 4
            wz, wzb = T(stk, "ewz", [128, 8, 2048], BF16)
            wzbs = [P.buf(f"wzp{q}") for q in range(4)]
            hTr = [T(stk, f"ehT{i}", [128, 8, TS], BF16) for i in range(2)]
            yfr = [T(stk, f"eyf{i}", [128, 2048], F32) for i in range(2)]
            ybr = [T(stk, f"eyb{i}", [128, 2048], F32) for i in range(2)]
            Xr = [T(stk, f"eX{i}", [128, 32, 64], BF16) for i in range(2)]
            xd, xdb = T(stk, "exd", [128, 32, 64], F32)
            zsr = [T(stk, f"ezs{i}", [128, 2048], F32) for i in range(2)]
            yznr = [T(stk, f"eyzn{i}", [128, 2048], BF16) for i in range(2)]
            ss2, ss2b = T(stk, "ess2", [128, 32], F32)
            yzT = [T(stk, f"eyzT{i}", [128, 16, TS], BF16) for i in range(2)]
            for q in range(4):
                P.dma("pool", wz[:, :, 512 * q:512 * q + 512], w_in_r[:, :, C_Z + 512 * q:C_Z + 512 * q + 512], w=[wzbs[q]])
            P.op("dve", lambda e: e.memset(ss2[:], 0.0), w=[ss2b])

            def load(t):
                i = t // cpt
                if t % cpt == 0:
                    hTt, hTtb = hTr[i % 2]
                    P.dma("sp", hTt[:], HT_d[:, :, 2 + TS * i:2 + TS * i + TS], w=[hTtb])
                yf, yfb = yfr[t % 2]; yb, ybb = ybr[t % 2]; X, Xb = Xr[t % 2]
                P.dma("sp", yf[:], YF_d[128 * t:128 * t + 128, :], w=[yfb])
                P.dma("act", yb[:], YB_d[128 * t:128 * t + 128, :], w=[ybb])
                P.dma("sp", X[:], XTM_d[128 * t:128 * t + 128, :].rearrange("p (h d) -> p h d", d=64), w=[Xb])
            def s1a(t):
                i = t // cpt; off = (t % cpt) * 128
                hTt, hTtb = hTr[i % 2]
                yf, yfb = yfr[t % 2]; yb, ybb = ybr[t % 2]; X, Xb = Xr[t % 2]
                zs, zsb = zsr[t % 2]
                P.op("pool", lambda e: e.tensor_tensor(out=yf[:], in0=yf[:], in1=yb[:], op=ALU.add), r=[ybb, yfb], w=[yfb])
                P.op("dve", lambda e: e.tensor_tensor(out=xd[:], in0=X[:], in1=vtm[:, O_SD:O_SD + 32].unsqueeze(2).broadcast_to([128, 32, 64]), op=ALU.mult),
                     r=[Xb, vtmb], w=[xdb])
                P.op("dve", lambda e: e.tensor_tensor(out=yf[:], in0=yf[:], in1=xd[:].rearrange("p a b -> p (a b)"), op=ALU.add), r=[xdb, yfb], w=[yfb])
                for q in range(4):
                    zp, zpb = PS[q % 4]
                    P.pe([(zp[:], [(hTt[:, k, off:off + 128], wz[:, k, 512 * q:512 * q + 512]) for k in range(8)])], r=[hTtb, wzbs[q]], w=[zpb])
                    P.op("act", lambda e, q=q, zp=zp: e.activation(out=zs[:, 512 * q:512 * q + 512], in_=zp[:], func=AF.Silu), r=[zpb], w=[zsb])

            def s1b(t):
                yf, yfb = yfr[t % 2]; zs, zsb = zsr[t % 2]; yzn, yznb = yznr[t % 2]
                P.op("dve", lambda e: e.tensor_tensor(out=zs[:], in0=yf[:], in1=zs[:], op=ALU.mult), r=[yfb, zsb], w=[zsb])
                sst = ss2[:, t:t + 1]
                P.op("act", lambda e: e.activation(out=yzn[:], in_=zs[:], func=AF.Square, accum_out=sst), r=[zsb], w=[yznb, ss2b])
                rstd_of(sst, ss2b, 2048)
                P.op("act", lambda e: e.activation(out=yzn[:], in_=zs[:], func=AF.Copy, scale=sst), r=[zsb, ss2b], w=[yznb])

            def s2(t):
                i = t // cpt; off = (t % cpt) * 128
                yzT_, yzTb = yzT[i % 2]; yzn, yznb = yznr[t % 2]
                for rd in range(2):
                    zt, ztb = PB[rd]
                    P.tr([(zt[:, 128 * j:128 * j + 128], yzn[:, 128 * (8 * rd + j):128 * (8 * rd + j) + 128]) for j in range(8)], idbf, r=[yznb, cbfb], w=[ztb])
                    P.op("dve", lambda e, rd=rd, zt=zt: e.tensor_copy(out=yzT_[:, 8 * rd:8 * rd + 8, off:off + 128],
                         in_=zt[:, 0:1024].rearrange("p (a b) -> p a b", b=128)), r=[ztb], w=[yzTb])
                if t % cpt == cpt - 1:
                    P.dma("sp", YZT_d.rearrange("(c f) t -> f c t", f=128)[:, :, TS * i:TS * i + TS], yzT_[:], r=[yzTb])

            load(0)
            load(1)
            s1a(0)
            s1b(0)
            for t in range(nch):
                if t + 2 < nch:
                    load(t + 2)
                if t + 1 < nch:
                    s1a(t + 1)
                s2(t)
                if t + 1 < nch:
                    s1b(t + 1)

        if upto >= 1:
            with ExitStack() as s1:
                hTc, hTcb = T(s1, "hTc", [128, 8, NCX + 4], BF16)
                cdb = [P.buf("XTMc_dram"), P.buf("BTMc_dram")]
                phase0("c0", ctx_d, NCX, lambda k: A1[:, k, 1:2], lambda k: modfm[:, k, 1:2], [A1b, modfmb], hTc, hTcb, s1)
                xbc_stage("c1", NCX, hTc, hTcb, s1, XTMc_d, BTMc_d, None, None, dtc, dtcb, 6, dbufs=cdb)
                for dirn in range(2):
                    ssd_pass(f"c2{dirn}", NCX, dirn, "state", XTMc_d, BTMc_d, None, None, dtc, dtcb, s1, dbufs=cdb)
                P.barrier()
            if debug:
                P.dma("sp", DBG_d[:, 2112:2112 + 2048], hst[0][0][:], r=hst[0][1])
                P.dma("sp", DBG_d[:, 4160:4160 + 2048], hst[1][0][:], r=hst[1][1])
                P.dma("sp", DBG_d[:, 6208:6208 + 128], dtc[:].rearrange("p a b -> p (a b)"), r=[dtcb])
        if upto >= 2:
            with ExitStack() as s2:
                hT, hTb = T(s2, "hT", [128, 8, NT + 4], BF16)
                with ExitStack() as s2a:
                    phase0("l0", x_d, NT, lambda k: A1[:, k, 0:1], lambda k: modfm[:, k, 0:1], [A1b, modfmb], hT, hTb, s2a)
                    P.dma("sp", HT_d, hT[:], r=[hTb])
                    P.barrier()
                with ExitStack() as s2b:
                    xbc_stage("l1", NT, hT, hTb, s2b, XTM_d, BTM_d, BTF_d, CTF_d, dtl, dtlb, 8)
                    P.barrier()
            if debug:
                P.dma("sp", DBG_d[:, 6336:6336 + 1024], dtl[:, 0:16, :].rearrange("p a b -> p (a b)"), r=[dtlb])
        if upto >= 3:
            with ExitStack() as s3:
                ssd_pass("f", NT, 0, "y", XTM_d, BTM_d, BTF_d, CTF_d, dtl, dtlb, s3, Y_d=YF_d)
                P.barrier()
        if upto >= 4:
            with ExitStack() as s4:
                ssd_pass("b", NT, 1, "y", XTM_d, BTM_d, BTF_d, CTF_d, dtl, dtlb, s4, Y_d=YB_d)
                P.barrier()

        gssd.close()
        if upto >= 4:
            with ExitStack() as s4e:
                dgt = [T(s4e, f"dgbuild{i}", [128, 31, 128], BF16) for i in range(2)]
                for ch in range(8):
                    dgc_, dgcb_ = dgt[ch % 2]
                    P.op("dve", lambda e, ch=ch, dgc_=dgc_: e.scalar_tensor_tensor(out=dgc_[:], in0=id32.unsqueeze(1).broadcast_to([128, 31, 128]), scalar=0.5,
                         in1=vfm[:, O_CDW + 31 * ch:O_CDW + 31 * ch + 31].unsqueeze(2).broadcast_to([128, 31, 128]), op0=ALU.mult, op1=ALU.mult), r=[c32b, vfmb], w=[dgcb_])
                    P.dma("act", DGC_d[ch].rearrange("p (a b) -> p a b", b=128), dgc_[:], r=[dgcb_])
                ssd_epilogue(s4e)
                P.barrier()
        if upto >= 5:
            with ExitStack() as s5:
                wgl, wglb = T(s5, "wgl", [128, 8, 2048], BF16)
                wco, wcob = T(s5, "wco", [128, 8, 1024], BF16)
                dgr = [T(s5, f"dgc{i}", [128, 31, 128], BF16) for i in range(3)]
                dgdb = [P.buf(f"dgc_dram{c}") for c in range(8)]
                hTr = [T(s5, f"cvhT{i}", [128, 8, 512], BF16) for i in range(2)]
                upad = [T(s5, f"upad{i}", [128, 8, 8, 94], BF16) for i in range(2)]
                sgr = [T(s5, f"cvsg{i}", [128, 512], F32) for i in range(2)]
                v32t = s5.enter_context(nc.sbuf_tensor("sb_v32", [128, 8, 512], F32)); v32bs = [P.buf(f"v32_{c}") for c in range(8)]
                vbft = s5.enter_context(nc.sbuf_tensor("sb_vbf", [128, 8, 512], BF16)); vbfbs = [P.buf(f"vbf_{c}") for c in range(8)]
                vsqt = s5.enter_context(nc.sbuf_tensor("sb_vsq", [128, 8, 512], BF16)); vsqbs = [P.buf(f"vsq_{c}") for c in range(8)]
                mean, meanb = T(s5, "mean", [128, 512], F32)
                var, varb = T(s5, "var", [128, 512], F32)
                tmpr = [T(s5, f"cvtmp{i}", [128, 512], F32) for i in range(2)]
                aat = s5.enter_context(nc.sbuf_tensor("sb_cvaa", [128, 8, 512], BF16)); aabs = [P.buf(f"aa_{c}") for c in range(8)]
                ucr = [T(s5, f"uc{i}", [128, 8, 512], BF16) for i in range(2)]
                wglbs = [P.buf(f"wglp{q}") for q in range(4)]
                wcobs = [P.buf(f"wcop{q}") for q in range(2)]
                for q in (0, 2, 1, 3):
                    P.dma("pool", wgl[:, :, 512 * q:512 * q + 512], w_in_r[:, :, C_GLU + 512 * q:C_GLU + 512 * q + 512], w=[wglbs[q]])
                for q in range(2):
                    P.dma("pool", wco[:, :, 512 * q:512 * q + 512], w_co_d.rearrange("(k p) c -> p k c", p=128)[:, :, 512 * q:512 * q + 512], w=[wcobs[q]])
                for i in range(2):
                    P.op("dve", lambda e, i=i: e.memset(upad[i][0][:], 0.0), w=[upad[i][1]])
                ntile = NT // 512
                cnt = {"dg": 0, "dgb": 0}

                def glu_mm(i, ch):
                    hTt, hTtb = hTr[i % 2]
                    p0, p0b = PS[ch % 2]; p1, p1b = PS[2 + ch % 2]
                    P.pe([(p0[:], [(wgl[:, k, 128 * ch:128 * ch + 128], hTt[:, k, :]) for k in range(8)])], r=[wglbs[ch // 4], hTtb], w=[p0b])
                    P.pe([(p1[:], [(wgl[:, k, 1024 + 128 * ch:1024 + 128 * ch + 128], hTt[:, k, :]) for k in range(8)])], r=[wglbs[2 + ch // 4], hTtb], w=[p1b])

                def build_dg(ch):
                    dgc, dgcb = dgr[cnt["dgb"] % 3]; cnt["dgb"] += 1
                    P.dma("sp", dgc[:], DGC_d[ch].rearrange("p (a b) -> p a b", b=128), w=[dgcb])

                pend = {}

                def headA(i, ch):
                    up, upb = upad[i % 2]
                    p0, p0b = PS[ch % 2]; p1, p1b = PS[2 + ch % 2]
                    sg, sgb = sgr[ch % 2]
                    dgc, dgcb = dgr[cnt["dg"] % 3]; cnt["dg"] += 1
                    P.op("act", lambda e: e.activation(out=sg[:], in_=p1[:], func=AF.Tanh, scale=0.5), r=[p1b], w=[sgb])
                    P.op("dve", lambda e: e.scalar_tensor_tensor(out=up[:, ch, :, 15:79], in0=sg[:].rearrange("p (a b) -> p a b", b=64), scalar=1.0,
                         in1=p0[:].rearrange("p (a b) -> p a b", b=64), op0=ALU.add, op1=ALU.mult), r=[p0b, sgb], w=[upb])
                    if ch < 7:
                        glu_mm(i, ch + 1)
                    build_dg((ch + 2) % 8)
                    pc, pcb = PS[4 + ch % 2]
                    P.pe([(pc[:], [(dgc[:, j, :], up[:, ch, :, j:j + 64]) for j in range(31)])], r=[dgcb, upb], w=[pcb])

                def headB(i, ch):
                    pc, pcb = PS[4 + ch % 2]
                    bcol = vfm[:, O_CDB + ch:O_CDB + ch + 1]
                    P.op("act", lambda e: e.activation(out=v32t[:, ch, :], in_=pc[:], func=AF.Identity, bias=bcol), r=[pcb, vfmb], w=[v32bs[ch]])
                    P.op("act", lambda e: e.activation(out=vsqt[:, ch, :], in_=pc[:], func=AF.Square, bias=bcol), r=[pcb, vfmb], w=[vsqbs[ch]])
                    P.op("dve", lambda e: e.tensor_copy(out=vbft[:, ch, :], in_=v32t[:, ch, :]), r=[v32bs[ch]], w=[vbfbs[ch]])

                def load(i):
                    hTt, hTtb = hTr[i % 2]
                    P.dma("sp", hTt[:], HT_d[:, :, 2 + 512 * i:2 + 512 * i + 512], w=[hTtb])

                load(0)
                glu_mm(0, 0)
                build_dg(0)
                build_dg(1)
                headA(0, 0)
                for ch in range(8):
                    headB(0, ch)
                    if ch < 7:
                        headA(0, ch + 1)
                for i in range(ntile):
                    uc, ucb = ucr[i % 2]
                    if i + 1 < ntile:
                        load(i + 1)
                    s4b, s4bb = PB[0]; s5b_, s5bb = PB[1]
                    s4f = s4b[:].bitcast(F32); s5f = s5b_[:].bitcast(F32)
                    P.pe([(s4f, [(onebf, vbft[:, ch, :]) for ch in range(8)])], r=[cbfb] + vbfbs, w=[s4bb])
                    P.pe([(s5f, [(onebf, vsqt[:, ch, :]) for ch in range(8)])], r=[cbfb] + vsqbs, w=[s5bb])
                    if i + 1 < ntile:
                        glu_mm(i + 1, 0)
                        headA(i + 1, 0)
                    tmp, tmpb = tmpr[0]
                    P.op("dve", lambda e: e.tensor_scalar_mul(out=mean[:], in0=s4f, scalar1=1.0 / 1024), r=[s4bb], w=[meanb])
                    P.op("dve", lambda e: e.tensor_tensor(out=tmp[:], in0=mean[:], in1=mean[:], op=ALU.mult), r=[meanb], w=[tmpb])
                    P.op("dve", lambda e: e.scalar_tensor_tensor(out=var[:], in0=s5f, scalar=1.0 / 1024, in1=tmp[:], op0=ALU.mult, op1=ALU.subtract), r=[s5bb, tmpb], w=[varb])
                    P.op("act", lambda e: e.activation(out=var[:], in_=var[:], func=AF.Sqrt, bias=EPS), r=[varb], w=[varb])
                    P.op("dve", lambda e: e.reciprocal(out=var[:], in_=var[:]), r=[varb], w=[varb])
                    for ch in range(8):
                        tmp, tmpb = tmpr[ch % 2]
                        P.op("dve", lambda e: e.tensor_tensor(out=tmp[:], in0=v32t[:, ch, :], in1=mean[:], op=ALU.subtract), r=[v32bs[ch], meanb], w=[tmpb])
                        P.op("dve", lambda e: e.tensor_tensor(out=tmp[:], in0=tmp[:], in1=var[:], op=ALU.mult), r=[tmpb, varb], w=[tmpb])
                        P.op("act", lambda e: e.activation(out=aat[:, ch, :], in_=tmp[:], func=AF.Silu, bias=vfm[:, O_CLB + ch:O_CLB + ch + 1],
                             scale=vfm[:, O_CLW + ch:O_CLW + ch + 1]), r=[tmpb, vfmb], w=[aabs[ch]])
                        if i + 1 < ntile:
                            headB(i + 1, ch)
                            if ch < 7:
                                headA(i + 1, ch + 1)
                    for dc in range(8):
                        pq, pqb = PS[4 + dc % 2]
                        P.pe([(pq[:], [(wco[:, ch, 128 * dc:128 * dc + 128], aat[:, ch, :]) for ch in range(8)])], r=[wcobs[dc // 4]] + aabs, w=[pqb])
                        P.op("act", lambda e, dc=dc, pq=pq: e.activation(out=uc[:, dc, :], in_=pq[:], func=AF.Identity, bias=vfm[:, O_BCO + dc:O_BCO + dc + 1]),
                             r=[pqb, vfmb], w=[ucb])
                    P.dma("sp", UC_d.rearrange("(c p) t -> p c t", p=128)[:, :, 512 * i:512 * i + 512], uc[:], r=[ucb])
                P.barrier()

        if upto >= 6:
            with ExitStack() as s6:
                wg, wgb = T(s6, "wg", [128, 8, 2048], BF16)
                wso, wsob = T(s6, "wso", [128, 16, 1024], BF16)
                wo, wob = T(s6, "wo", [128, 8, 1024], BF16)
                wobs = [P.buf(f"wop{q}") for q in range(2)]
                hTr = [T(s6, f"mhT{i}", [128, 8, 512], BF16) for i in range(2)]
                yzr = [T(s6, f"myz{i}", [128, 16, 512], BF16) for i in range(2)]
                ucr = [T(s6, f"muc{i}", [128, 8, 512], BF16) for i in range(2)]
                gcr = [T(s6, f"gc{i}", [128, 512], BF16) for i in range(2)]
                gsr = [T(s6, f"gs{i}", [128, 512], BF16) for i in range(2)]
                m1r = [T(s6, f"m1{i}", [128, 512], F32) for i in range(1)]
                m2, m2b = T(s6, "m2", [128, 512], F32)
                mg, mgb = T(s6, "mg", [128, 8, 512], BF16)
                xr = [T(s6, f"mx{i}", [128, D], F32) for i in range(2)]
                x1r = [T(s6, f"mx1{i}", [128, D], F32) for i in range(2)]
                xn, xnb = T(s6, "mxn", [128, D], BF16)
                ss, ssb = T(s6, "mss", [128, 32], F32)
                h2r = [T(s6, f"mh2{i}", [128, 8, 512], BF16) for i in range(1)]
                wgbs = [P.buf(f"wgp{q}") for q in range(4)]
                wsobs = [P.buf(f"wsop{q}") for q in range(2)]
                def ld_wg(q):
                    P.dma("pool", wg[:, :, 512 * q:512 * q + 512], w_in_r[:, :, C_GATE + 512 * q:C_GATE + 512 * q + 512], w=[wgbs[q]])

                def ld_wso(q):
                    P.dma("pool", wso[:, :, 512 * q:512 * q + 512], w_so_d.rearrange("(k p) c -> p k c", p=128)[:, :, 512 * q:512 * q + 512], w=[wsobs[q]])

                def ld_wo(q):
                    P.dma("pool", wo[:, :, 512 * q:512 * q + 512], w_o_d.rearrange("(k p) c -> p k c", p=128)[:, :, 512 * q:512 * q + 512], w=[wobs[q]])
                ld_wg(0); ld_wg(2); ld_wso(0); ld_wg(1); ld_wg(3); ld_wso(1); ld_wo(0); ld_wo(1)
                for q in range(2):
                    P.op("dve", lambda e, q=q: e.tensor_tensor(out=wso[:, :, 512 * q:512 * q + 512], in0=wso[:, :, 512 * q:512 * q + 512],
                         in1=vfm[:, O_SNW:O_SNW + 16].unsqueeze(2).broadcast_to([128, 16, 512]), op=ALU.mult), r=[vfmb, wsobs[q]], w=[wsobs[q]])
                P.op("dve", lambda e: e.memset(ss[:], 0.0), w=[ssb])
                ntile = NT // 512

                def load(i):
                    hTt, hTtb = hTr[i % 2]; yz, yzb = yzr[i % 2]; uc, ucb = ucr[i % 2]
                    P.dma("sp", hTt[:], HT_d[:, :, 2 + 512 * i:2 + 512 * i + 512], w=[hTtb])
                    P.dma("sp", yz[:], YZT_d.rearrange("(c f) t -> f c t", f=128)[:, :, 512 * i:512 * i + 512], w=[yzb])
                    P.dma("sp", uc[:], UC_d.rearrange("(c p) t -> p c t", p=128)[:, :, 512 * i:512 * i + 512], w=[ucb])
                load(0)
                s5pend = []
                for i in range(ntile):
                    hTt, hTtb = hTr[i % 2]; yz, yzb = yzr[i % 2]; uc, ucb = ucr[i % 2]; h2, h2b = h2r[0]
                    if i + 1 < ntile:
                        load(i + 1)
                    for dc in range(8):
                        pgc, pgcb = PS[0]; pgs, pgsb = PS[1]
                        gc_, gcb = gcr[dc % 2]; gs_, gsb = gsr[dc % 2]; m1, m1b = m1r[0]
                        P.pe([(pgc[:], [(wg[:, k, 128 * dc:128 * dc + 128], hTt[:, k, :]) for k in range(8)])], r=[wgbs[dc // 4], hTtb], w=[pgcb])
                        P.pe([(pgs[:], [(wg[:, k, 1024 + 128 * dc:1024 + 128 * dc + 128], hTt[:, k, :]) for k in range(8)])], r=[wgbs[2 + dc // 4], hTtb], w=[pgsb])
                        pu, pub = PS[2 + dc % 2]
                        P.pe([(pu[:], [(wso[:, c, 128 * dc:128 * dc + 128], yz[:, c, :]) for c in range(16)])], r=[wsobs[dc // 4], yzb], w=[pub])
                        if dc == 0 and s5pend:
                            s5pend.pop()()
                        P.op("act", lambda e: e.activation(out=gc_[:], in_=pgc[:], func=AF.Sigmoid), r=[pgcb], w=[gcb])
                        P.op("act", lambda e: e.activation(out=gs_[:], in_=pgs[:], func=AF.Sigmoid), r=[pgsb], w=[gsb])
                        P.op("dve", lambda e: e.tensor_tensor(out=m1[:], in0=pu[:], in1=gs_[:], op=ALU.mult), r=[pub, gsb], w=[m1b])
                        P.op("dve", lambda e: e.tensor_tensor(out=m2[:], in0=uc[:, dc, :], in1=gc_[:], op=ALU.mult), r=[ucb, gcb], w=[m2b])
                        P.op("dve", lambda e: e.tensor_tensor(out=mg[:, dc, :], in0=m1[:], in1=m2[:], op=ALU.add), r=[m1b, m2b], w=[mgb])
                    xnr = [(xn[:], xnb), (m2[:].bitcast(BF16), m2b)]

                    def partA(cq):
                        t = 4 * i + cq
                        x_, xb = xr[cq % 2]; x1, x1b = x1r[cq % 2]
                        xn_, xn_b = xnr[cq % 2]
                        P.dma("sp", x_[:], x_d[128 * t:128 * t + 128, :], w=[xb])
                        for half in range(2):
                            pm, pmb = PB[half]
                            pmf = pm[:].bitcast(F32)
                            P.pe([(pmf, [(mg[:, dc, 128 * cq:128 * cq + 128], wo[:, dc, 512 * half:512 * half + 512]) for dc in range(8)])], r=[mgb, wobs[half]], w=[pmb])
                            P.op("dve", lambda e: e.tensor_tensor(out=x1[:, 512 * half:512 * half + 512], in0=pmf, in1=g1bc[:, 512 * half:512 * half + 512], op=ALU.mult), r=[pmb, g1b], w=[x1b])
                            P.op("dve", lambda e: e.tensor_tensor(out=x1[:, 512 * half:512 * half + 512], in0=x1[:, 512 * half:512 * half + 512], in1=x_[:, 512 * half:512 * half + 512], op=ALU.add),
                                 r=[xb, x1b], w=[x1b])
                        P.dma("sp", X1_d[128 * t:128 * t + 128, :], x1[:], r=[x1b])
                        sst = ss[:, t:t + 1]
                        P.op("act", lambda e: e.activation(out=xn_, in_=x1[:], func=AF.Square, accum_out=sst), r=[x1b], w=[xn_b, ssb])
                        rstd_of(sst, ssb, D)
                        P.op("act", lambda e: e.activation(out=xn_, in_=x1[:], func=AF.Copy, scale=sst), r=[x1b, ssb], w=[xn_b])

                    def partB(cq):
                        xn_, xn_b = xnr[cq % 2]
                        pt, ptb = PS[4 + cq % 2]
                        ptv = pt[:].bitcast(BF16)
                        P.tr([(ptv[:, 128 * k:128 * k + 128], xn_[:, 128 * k:128 * k + 128]) for k in range(8)], idbf, r=[xn_b, cbfb], w=[ptb])
                        for k in range(8):
                            P.op("dve", lambda e, k=k: e.tensor_scalar(out=h2[:, k, 128 * cq:128 * cq + 128], in0=ptv[:, 128 * k:128 * k + 128],
                                 scalar1=A2[:, k, 0:1], scalar2=modfm[:, 16 + k, 0:1], op0=ALU.mult, op1=ALU.add), r=[ptb, A2b, modfmb], w=[h2b])

                    for cq in range(4):
                        partA(cq)
                        if cq >= 1:
                            partB(cq - 1)

                    def tail(i=i, partB=partB, h2=h2, h2b=h2b):
                        partB(3)
                        P.dma("sp", H2T_d[:, :, 512 * i:512 * i + 512], h2[:], r=[h2b])
                    s5pend.append(tail)
                s5pend.pop()()
                P.barrier()

        outbufs = []
        if upto >= 7:
            with ExitStack() as s7:
                w1, w1b = T(s7, "w1", [128, 8, 4096], BF16)
                w2, w2b = T(s7, "w2", [128, 32, 1024], BF16)
                w1bs = [P.buf(f"w1p{q}") for q in range(8)]
                w2bs = [[P.buf(f"w2p{q}{hh}") for hh in range(2)] for q in range(2)]
                h2r = [T(s7, f"fh2{i}", [128, 8, 512], BF16) for i in range(1)]
                hid, hidb = T(s7, "hid", [128, 32, 512], BF16)
                rl = [T(s7, f"rl{i}", [128, 512], F32) for i in range(1)]
                x1r = [T(s7, f"fx1{i}", [128, D], F32) for i in range(1)]
                x2, x2b = T(s7, "fx2", [128, D], F32)
                orr = [(x2, x2b)]
                ss, ssb = T(s7, "fss", [128, 32], F32)
                for q in range(8):
                    P.dma("pool", w1[:, :, 512 * q:512 * q + 512], w_m1_d.rearrange("(k p) c -> p k c", p=128)[:, :, 512 * q:512 * q + 512], w=[w1bs[q]])
                for q in range(2):
                    for hh in range(2):
                        P.dma("pool", w2[:, 16 * hh:16 * hh + 16, 512 * q:512 * q + 512],
                              w_m2_d.rearrange("(k p) c -> p k c", p=128)[:, 16 * hh:16 * hh + 16, 512 * q:512 * q + 512], w=[w2bs[q][hh]])
                P.op("dve", lambda e: e.memset(ss[:], 0.0), w=[ssb])
                h2, h2b = h2r[0]
                P.dma("sp", h2[:], H2T_d[:, :, 0:512], w=[h2b])
                for i in range(NT // 512):
                    for f in range(32):
                        pf, pfb = PS[f % 2]; r_, rb = rl[0]
                        P.pe([(pf[:], [(w1[:, k, 128 * f:128 * f + 128], h2[:, k, :]) for k in range(8)])], r=[w1bs[f // 4], h2b], w=[pfb])
                        P.op("act", lambda e, pf=pf, r_=r_: e.activation(out=r_[:], in_=pf[:], func=AF.Relu), r=[pfb], w=[rb])
                        P.op("dve", lambda e, f=f, r_=r_: e.tensor_tensor(out=hid[:, f, :], in0=r_[:], in1=r_[:], op=ALU.mult), r=[rb], w=[hidb])
                    if i + 1 < NT // 512:
                        P.dma("sp", h2[:], H2T_d[:, :, 512 * (i + 1):512 * (i + 1) + 512], w=[h2b])
                    for cq in range(4):
                        t = 4 * i + cq
                        x1, x1b = x1r[0]; o_, ob = orr[0]
                        P.dma("sp", x1[:], X1_d[128 * t:128 * t + 128, :], w=[x1b])
                        for half in range(2):
                            pm, pmb = PS[2 + half]
                            P.pe([(pm[:], [(hid[:, f, 128 * cq:128 * cq + 128], w2[:, f, 512 * half:512 * half + 512]) for f in range(32)])], r=[hidb] + w2bs[half], w=[pmb])
                            P.op("dve", lambda e, pm=pm, half=half: e.tensor_tensor(out=x2[:, 512 * half:512 * half + 512], in0=pm[:], in1=g5bc[:, 512 * half:512 * half + 512], op=ALU.mult), r=[pmb, g5b], w=[x2b])
                            P.op("dve", lambda e, half=half, x1=x1: e.tensor_tensor(out=x2[:, 512 * half:512 * half + 512], in0=x2[:, 512 * half:512 * half + 512], in1=x1[:, 512 * half:512 * half + 512], op=ALU.add),
                                 r=[x1b, x2b], w=[x2b])
                        sst = ss[:, t:t + 1]
                        jk, jkb = rl[0]
                        P.op("act", lambda e: e.activation(out=jk[:].bitcast(BF16), in_=x2[:], func=AF.Square, accum_out=sst), r=[x2b], w=[jkb, ssb])
                        rstd_of(sst, ssb, D)
                        P.op("dve", lambda e, o_=o_: e.scalar_tensor_tensor(out=o_[:], in0=x2[:], scalar=sst, in1=vtm[:, O_FNW:O_FNW + 1024], op0=ALU.mult, op1=ALU.mult),
                             r=[x2b, ssb, vtmb], w=[ob])
                        P.dma("sp", out_d[128 * t:128 * t + 128, :], o_[:], r=[ob])
                P.barrier()
        P.barrier()
    return nc


def _prep(inputs):
    f = lambda a: np.ascontiguousarray(np.asarray(a, dtype=np.float32))
    fm = lambda v: f(v).reshape(-1, 128).T
    c = f(inputs["c"]); cctx = f(inputs["c_ctx"])
    vfm = np.zeros((128, NV), np.float32)
    vfm[:, O_BADA:O_BADA + 48] = fm(inputs["b_ada"][0])
    vfm[:, O_N1:O_N1 + 8] = fm(inputs["norm1_w"][0]); vfm[:, O_N2:O_N2 + 8] = fm(inputs["norm2_w"][0])
    cdw = f(inputs["conv_dw_w"][0])
    vfm[:, O_CDW:O_CDW + 248] = cdw.reshape(31, 8, 128).transpose(2, 1, 0).reshape(128, 248)
    vfm[:, O_CDB:O_CDB + 8] = fm(inputs["conv_dw_b"][0]); vfm[:, O_CLW:O_CLW + 8] = fm(inputs["conv_ln_w"][0])
    vfm[:, O_CLB:O_CLB + 8] = fm(inputs["conv_ln_b"][0]); vfm[:, O_BCO:O_BCO + 8] = fm(inputs["b_conv_out"][0])
    scw = f(inputs["ssm_conv_w"][0])
    vfm[:, O_SCW:O_SCW + 160] = scw.reshape(5, 32, 128).transpose(2, 1, 0).reshape(128, 160)
    vfm[:, O_SCB:O_SCB + 32] = fm(inputs["ssm_conv_b"][0]); vfm[:, O_SNW:O_SNW + 16] = fm(inputs["ssm_norm_w"][0])
    row = np.zeros((NW,), np.float32)
    ba = f(inputs["b_ada"][0])
    rowa = np.concatenate([ba[2048:3072], ba[5120:6144]])
    vtma = np.ascontiguousarray(np.broadcast_to(rowa[None, :], (128, NWA)))
    row[O_FNW:O_FNW + 1024] = f(inputs["final_norm_w"])
    row[O_DTB:O_DTB + 64] = f(inputs["ssm_dt_bias"][0]).reshape(64); row[O_ALOG:O_ALOG + 64] = f(inputs["ssm_a_log"][0]).reshape(64)
    row[O_SD:O_SD + 32] = f(inputs["ssm_d"][0])
    vtm = np.ascontiguousarray(np.broadcast_to(row[None, :], (128, NW)))
    idx = np.arange(128)
    j, k = idx[:, None], idx[None, :]
    consts = np.concatenate([(j == k), (j <= k), (j >= k), (j > k), (j < k), np.ones((128, 128), bool)], axis=1).astype(np.float32)
    negf = np.where(k >= j, 0.0, -30000.0).astype(np.float32)
    negb = np.where(k <= j, 0.0, -30000.0).astype(np.float32)
    consts = np.concatenate([consts, negf, negb], axis=1)
    sel = np.zeros((128, 32, 128), np.float32)
    for p in range(96):
        sel[p, p % 32, :] = 1.0
    sel = sel.reshape(128, 4096)
    shared = dict(vfm=vfm, vtm=vtm, vtma=vtma, consts=consts, sel=sel, w_ada=f(inputs["w_ada"][0]), w_in=f(inputs["w_in"][0]), w_conv_out=f(inputs["w_conv_out"][0]),
                  w_ssm_out=f(inputs["w_ssm_out"][0]), w_o=f(inputs["w_o"][0]), w_mlp1=f(inputs["w_mlp1"][0]), w_mlp2=f(inputs["w_mlp2"][0]))
    maps = []
    x = inputs["x"]; ctx = inputs["ctx"]
    for b in range(x.shape[0]):
        cT = np.stack([c[b].reshape(8, 128).T, cctx.reshape(8, 128).T], axis=2).reshape(128, 16)
        m = dict(shared)
        m["x"] = f(x[b]); m["ctx"] = f(ctx[b]); m["cT"] = np.ascontiguousarray(cT)
        maps.append(m)
    return maps


def kernel(**inputs):
    maps = _prep(inputs)
    nc = build()
    res = run_bass_kernel_spmd(nc, maps, core_ids=list(range(len(maps))))
    return np.stack([np.asarray(r["out"], dtype=np.float32) for r in res.results], axis=0)
```

```python
import numpy as np
import concourse.bass as bass
import concourse.mybir as mybir
from concourse.bass_utils import run_bass_kernel_spmd
from contextlib import ExitStack

F32 = mybir.dt.float32
BF16 = mybir.dt.bfloat16
AF = mybir.ActivationFunctionType
ALU = mybir.AluOpType

D = 1024
NT = 4096
NCX = 256
PROJ = 10304
C_DT, C_Z, C_GLU, C_GATE = 4096, 4160, 6208, 8256
EPS = 1e-6
O_BADA, O_N1, O_N2, O_CDW, O_CDB, O_CLW, O_CLB, O_BCO, O_SCW, O_SCB, O_SNW, NV = 0, 48, 56, 64, 312, 320, 328, 336, 344, 504, 536, 552
O_BG1, O_BG5, NWA = 0, 1024, 2048
O_FNW, O_DTB, O_ALOG, O_SD, NW = 0, 1024, 1088, 1152, 1184
K_ID, K_U, K_L, K_SL, K_SU, K_ONE, K_NEGF, K_NEGB, NK = 0, 128, 256, 384, 512, 640, 768, 896, 1024


class Sem:
    __slots__ = ("h", "n", "dma")


class Buf:
    __slots__ = ("name", "w", "r", "dsem", "excl")


class Prog:
    def __init__(self, nc, st):
        self.nc = nc
        self.st = st
        self.k = 0
        self.eng = {"pe": nc.tensor, "act": nc.scalar, "dve": nc.vector, "pool": nc.gpsimd, "sp": nc.sync}
        self.esem = {e: self.new_sem(False) for e in ("pe", "act", "dve", "pool")}
        self.seen = {e: {} for e in self.eng}
        self.dpool = [self.new_sem(True) for _ in range(64)]
        self.di = 0
        self.allsems = []

    def new_sem(self, dma):
        s = Sem()
        s.h = self.st.enter_context(self.nc.semaphore(f"sm{self.k}"))
        self.k += 1
        s.n = 0
        s.dma = dma
        return s

    def buf(self, name):
        b = Buf()
        b.name = name
        b.w = None
        b.r = {}
        b.excl = name.startswith("ps") or name.startswith("pb")
        b.dsem = None
        return b

    def _dsem(self, b):
        if b.dsem is None:
            b.dsem = self.dpool[self.di % len(self.dpool)]
            self.di += 1
        return b.dsem

    def _waits(self, e, r, w):
        deps = {}

        def add(s, v):
            if deps.get(s, 0) < v:
                deps[s] = v
        own = self.esem.get(e)
        for b in r:
            if b.w:
                add(*b.w)
            if b.excl:
                for s, v in b.r.items():
                    if s is not own:
                        add(s, v)
        for b in w:
            if b.w:
                add(*b.w)
            for s, v in b.r.items():
                add(s, v)
        eng = self.eng[e]
        seen = self.seen[e]
        for s, v in deps.items():
            if seen.get(s, 0) >= v:
                continue
            if s.dma:
                v = s.n
            eng.wait_ge(s.h, v)
            seen[s] = v

    def _record(self, ev, r, w):
        s, v = ev
        for b in r:
            if b.r.get(s, 0) < v:
                b.r[s] = v
        for b in w:
            b.w = ev
            b.r = {}

    def op(self, e, fn, r=(), w=()):
        self._waits(e, r, w)
        ins = fn(self.eng[e])
        sem = self.esem[e]
        if sem.n >= 30000:
            sem = self.esem[e] = self.new_sem(False)
        sem.n += 1
        ins.then_inc(sem.h, 1)
        self._record((sem, sem.n), r, w)

    def dma(self, q, out, in_, r=(), w=(), sb=None):
        self._waits(q, r, w)
        ins = self.eng[q].dma_start(out=out, in_=in_)
        s = self._dsem(sb or (list(w) + list(r))[0])
        s.n += 16
        ins.then_inc(s.h, 16)
        self._record((s, s.n), r, w)

    def pe(self, groups, r, w):
        def fn(eng):
            ins = None
            for out_ap, pairs in groups:
                n = len(pairs)
                for i, (l, rr) in enumerate(pairs):
                    ins = eng.matmul(out_ap, l, rr, start=(i == 0), stop=(i == n - 1))
            return ins
        self.op("pe", fn, r=r, w=w)

    def pe_raw(self, mms, r, w):
        def fn(eng):
            ins = None
            for out_ap, l, rr, st_, sp_ in mms:
                ins = eng.matmul(out_ap, l, rr, start=st_, stop=sp_, skip_group_check=True)
            return ins
        self.op("pe", fn, r=r, w=w)

    def tr(self, items, ident, r, w):
        def fn(eng):
            ins = None
            for out_ap, in_ap in items:
                ins = eng.transpose(out_ap, in_ap, ident)
            return ins
        self.op("pe", fn, r=r, w=w)

    def barrier(self):
        sems = [s for s in self.esem.values()] + self.dpool
        for e in self.eng:
            eng = self.eng[e]
            seen = self.seen[e]
            for s in sems:
                if s.n > 0 and seen.get(s, 0) < s.n:
                    eng.wait_ge(s.h, s.n)
                    seen[s] = s.n


def build(debug=False, upto=99):
    nc = bass.Bass("TRN2", target_bir_lowering=False)

    def din(name, shape):
        return nc.dram_tensor(name, shape, F32, kind="ExternalInput").ap()

    skind = "ExternalOutput" if debug else "Internal"

    def dscr(name, shape, dt):
        return nc.dram_tensor(name, shape, dt, kind=skind).ap()

    x_d = din("x", [NT, D]); ctx_d = din("ctx", [NCX, D]); cT_d = din("cT", [128, 16])
    vfm_d = din("vfm", [128, NV]); vtm_d = din("vtm", [128, NW]); vtma_d = din("vtma", [128, NWA]); consts_d = din("consts", [128, NK]); sel_d = din("sel", [128, 4096])
    w_ada_d = din("w_ada", [D, 6 * D]); w_in_d = din("w_in", [D, PROJ]); w_co_d = din("w_conv_out", [D, D])
    w_so_d = din("w_ssm_out", [2 * D, D]); w_o_d = din("w_o", [D, D]); w_m1_d = din("w_mlp1", [D, 4 * D]); w_m2_d = din("w_mlp2", [4 * D, D])
    out_d = nc.dram_tensor("out", [NT, D], F32, kind="ExternalOutput").ap()

    HT_d = dscr("HT", [128, 8, NT + 4], BF16)
    XTM_d = dscr("XTM", [NT, 2048], BF16); BTM_d = dscr("BTM", [NT, 1024], BF16)
    BTF_d = dscr("BTF", [1024, NT], BF16); CTF_d = dscr("CTF", [1024, NT], BF16)
    XTMc_d = dscr("XTMc", [NCX, 2048], BF16); BTMc_d = dscr("BTMc", [NCX, 1024], BF16)
    YF_d = dscr("YF", [NT, 2048], F32); YB_d = dscr("YB", [NT, 2048], F32); YZT_d = dscr("YZT", [2048, NT], BF16)
    UC_d = dscr("UC", [1024, NT], BF16); X1_d = dscr("X1", [NT, D], F32); H2T_d = dscr("H2T", [128, 8, NT], BF16)
    DGC_d = dscr("DGC", [8, 128, 31 * 128], BF16)
    DBG_d = nc.dram_tensor("DBG", [128, 8192], F32, kind="ExternalOutput").ap() if debug else None

    w_in_r = w_in_d.rearrange("(k p) c -> p k c", p=128)

    with ExitStack() as st:
        E = st.enter_context
        P = Prog(nc, st)

        def T(stack, name, shape, dt):
            t = stack.enter_context(nc.sbuf_tensor("sb_" + name, shape, dt))
            return t, P.buf(name)

        PS = []
        for i in range(6):
            PS.append((E(nc.psum_tensor(f"ps{i}", [128, 512], F32)), P.buf(f"ps{i}")))
        PB = []
        for i in range(2):
            PB.append((E(nc.psum_tensor(f"pb{i}", [128, 1024], BF16)), P.buf(f"pb{i}")))

        c32, c32b = T(st, "c32", [128, NK], F32)
        cbf, cbfb = T(st, "cbf", [128, NK], BF16)
        vfm, vfmb = T(st, "vfm", [128, NV], F32)
        vtm, vtmb = T(st, "vtm", [128, NW], F32)
        cT, cTb = T(st, "cT", [128, 8, 2], F32)
        sc, scb_ = T(st, "sc", [128, 8, 2], F32)
        modfm, modfmb = T(st, "modfm", [128, 32, 2], F32)
        g1bc, g1b = T(st, "g1bc", [128, 1024], F32)
        g5bc, g5b = T(st, "g5bc", [128, 1024], F32)
        A1, A1b = T(st, "A1", [128, 8, 2], F32)
        A2, A2b = T(st, "A2", [128, 8, 1], F32)
        abc, abcb = T(st, "abc", [128, 64], F32)
        gssd = ExitStack()
        dtl, dtlb = T(gssd, "dtl", [128, 32, 64], F32)
        dtc, dtcb = T(gssd, "dtc", [128, 2, 64], F32)
        hst = [(T(gssd, f"hst{i}", [128, 2048], F32)[0], [P.buf(f"hst{i}_{g}") for g in range(8)]) for i in range(2)]
        hbf = [(T(gssd, f"hbf{i}", [128, 2048], BF16)[0], [P.buf(f"hbf{i}_{g}") for g in range(8)]) for i in range(2)]
        SEL, SELb = T(gssd, "SEL", [128, 32, 128], BF16)
        P.dma("pool", SEL[:], sel_d.rearrange("p (a b) -> p a b", b=128), w=[SELb])

        P.dma("sp", c32[:], consts_d, w=[c32b])
        P.dma("pool", cbf[:], consts_d, w=[cbfb])
        P.dma("sp", vfm[:], vfm_d, w=[vfmb])
        P.dma("sp", vtm[:], vtm_d, w=[vtmb])
        P.dma("sp", cT[:], cT_d.rearrange("p (k t) -> p k t", t=2), w=[cTb])

        id32 = c32[:, K_ID:K_ID + 128]; idbf = cbf[:, K_ID:K_ID + 128]
        U32 = c32[:, K_U:K_U + 128]; L32 = c32[:, K_L:K_L + 128]
        SL32 = c32[:, K_SL:K_SL + 128]; SU32 = c32[:, K_SU:K_SU + 128]
        one32 = c32[:, K_ONE:K_ONE + 128]; onebf = cbf[:, K_ONE:K_ONE + 128]
        Ubf = cbf[:, K_U:K_U + 128]; Lbf = cbf[:, K_L:K_L + 128]

        with ExitStack() as s0:
            slabs = [T(s0, f"adaslab{i}", [128, 8, 512], F32) for i in range(2)]
            scbc, scbcb = T(s0, "scbc", [128, 8, 128], F32)
            vtma, vtmab = T(s0, "vtma", [128, NWA], F32)
            P.dma("sp", vtma[:], vtma_d, w=[vtmab])
            P.op("act", lambda e: e.activation(out=sc[:], in_=cT[:], func=AF.Silu), r=[cTb], w=[scb_])
            P.op("dve", lambda e: e.tensor_copy(out=scbc[:], in_=sc[:, :, 0].unsqueeze(2).broadcast_to([128, 8, 128])), r=[scb_], w=[scbcb])
            w_ada_r = w_ada_d.rearrange("(k p) c -> p k c", p=128)
            fmbase = {0: 0, 1: 8, 3: 16, 4: 24}
            for s in range(12):
                sl, slb = slabs[s % 2]
                P.dma("sp", sl[:], w_ada_r[:, :, 512 * s:512 * s + 512], w=[slb])
                m, half = s // 2, s % 2
                ps, psb = PS[s % 2]
                if m in (2, 5):
                    P.pe([(ps[:], [(scbc[:, k, :], sl[:, k, :]) for k in range(8)])], r=[scbcb, slb], w=[psb])
                    gt, gtb, ob = (g1bc, g1b, O_BG1) if m == 2 else (g5bc, g5b, O_BG5)
                    P.op("dve", lambda e, gt=gt, ps=ps, ob=ob, half=half: e.tensor_tensor(
                        out=gt[:, 512 * half:512 * half + 512], in0=ps[:], in1=vtma[:, ob + 512 * half:ob + 512 * half + 512], op=ALU.add),
                        r=[psb, vtmab], w=[gtb])
                else:
                    P.pe([(ps[:, 2 * j:2 * j + 2], [(sl[:, k, 128 * j:128 * j + 128], sc[:, k, :]) for k in range(8)]) for j in range(4)],
                         r=[scb_, slb], w=[psb])
                    for j in range(4):
                        ci = fmbase[m] + half * 4 + j
                        col = O_BADA + m * 8 + half * 4 + j
                        P.op("dve", lambda e, ps=ps, j=j, ci=ci, col=col: e.tensor_scalar_add(
                            out=modfm[:, ci, :], in0=ps[:, 2 * j:2 * j + 2], scalar1=vfm[:, col:col + 1]), r=[psb, vfmb], w=[modfmb])
            P.op("dve", lambda e: e.scalar_tensor_tensor(out=A1[:], in0=modfm[:, 8:16, :], scalar=1.0,
                 in1=vfm[:, O_N1:O_N1 + 8].unsqueeze(2).broadcast_to([128, 8, 2]), op0=ALU.add, op1=ALU.mult), r=[modfmb, vfmb], w=[A1b])
            P.op("dve", lambda e: e.scalar_tensor_tensor(out=A2[:], in0=modfm[:, 24:32, 0:1], scalar=1.0,
                 in1=vfm[:, O_N2:O_N2 + 8].unsqueeze(2), op0=ALU.add, op1=ALU.mult), r=[modfmb, vfmb], w=[A2b])
            P.op("act", lambda e: e.activation(out=abc[:], in_=vtm[:, O_ALOG:O_ALOG + 64], func=AF.Exp), r=[vtmb], w=[abcb])
            P.op("dve", lambda e: e.tensor_scalar_mul(out=abc[:], in0=abc[:], scalar1=-1.0), r=[abcb], w=[abcb])
            for i in range(2):
                P.op("dve", lambda e, i=i: e.memset(hst[i][0][:], 0.0), w=hst[i][1])
                P.op("dve", lambda e, i=i: e.memset(hbf[i][0][:], 0.0), w=hbf[i][1])
            P.barrier()
        if debug:
            P.dma("sp", DBG_d[:, 0:64], modfm[:].rearrange("p a b -> p (a b)"), r=[modfmb])
            P.dma("sp", DBG_d[:, 64:64 + 1024], g1bc[:], r=[g1b])
            P.dma("sp", DBG_d[:, 1088:1088 + 1024], g5bc[:], r=[g5b])

        def rstd_of(ss_ap, ssb, n_feat):
            P.op("act", lambda e: e.activation(out=ss_ap, in_=ss_ap, func=AF.Sqrt, bias=EPS, scale=1.0 / n_feat), r=[ssb], w=[ssb])
            P.op("dve", lambda e: e.reciprocal(out=ss_ap, in_=ss_ap), r=[ssb], w=[ssb])

        def phase0(tag, src_d, n, Aap, Shap, Abufs, hT, hTb, stk):
            xr = [T(stk, f"{tag}x{i}", [128, D], F32) for i in range(3)]
            xn = [T(stk, f"{tag}xn{i}", [128, D], BF16) for i in range(2)]
            junk, junkb = T(stk, f"{tag}junk", [128, D], BF16)
            ss, ssb = T(stk, f"{tag}ss", [128, 32], F32)
            ssbs = [P.buf(f"{tag}ss{t}") for t in range(n // 128)]
            P.op("dve", lambda e: e.memset(ss[:], 0.0), w=ssbs)
            P.op("dve", lambda e: e.memset(hT[:, :, 0:2], 0.0), w=[hTb])
            P.op("dve", lambda e: e.memset(hT[:, :, n + 2:n + 4], 0.0), w=[hTb])
            def phL(t):
                x_, xb = xr[t % 3]
                P.dma("sp" if t % 2 == 0 else "act", x_[:], src_d[128 * t:128 * t + 128, :], w=[xb])

            def phA(t):
                x_, xb = xr[t % 3]
                sst = ss[:, t:t + 1]
                P.op("act", lambda e: e.activation(out=junk[:], in_=x_[:], func=AF.Square, accum_out=sst), r=[xb], w=[junkb, ssbs[t]])
                rstd_of(sst, ssbs[t], D)

            def phB(t):
                x_, xb = xr[t % 3]; xn_, xnb = xn[t % 2]
                sst = ss[:, t:t + 1]
                P.op("act", lambda e: e.activation(out=xn_[:], in_=x_[:], func=AF.Copy, scale=sst), r=[xb, ssbs[t]], w=[xnb])
                pb, pbb = PB[t % 2]
                P.tr([(pb[:, 128 * k:128 * k + 128], xn_[:, 128 * k:128 * k + 128]) for k in range(8)], idbf, r=[xnb, cbfb], w=[pbb])
                for k in range(8):
                    P.op("dve", lambda e, k=k: e.tensor_scalar(out=hT[:, k, 2 + 128 * t:2 + 128 * t + 128], in0=pb[:, 128 * k:128 * k + 128],
                         scalar1=Aap(k), scalar2=Shap(k), op0=ALU.mult, op1=ALU.add), r=[pbb] + Abufs, w=[hTb])

            ncht = n // 128
            phL(0)
            if ncht > 1:
                phL(1)
            phA(0)
            for t in range(ncht):
                if t + 2 < ncht:
                    phL(t + 2)
                if t + 1 < ncht:
                    phA(t + 1)
                phB(t)

        def xbc_stage(tag, n, hT, hTb, stk, XTM, BTM, BTF, CTF, dt_all, dt_allb, nslab, dbufs=None):
            TS = min(512, n); ntile = n // TS; nch = n // 128
            wsl = [T(stk, f"{tag}wsl{i}", [128, 8, 512], BF16) for i in range(2)]
            wdt, wdtb = T(stk, f"{tag}wdt", [128, 8, 64], BF16)
            dg = [T(stk, f"{tag}dg{i}", [128, 5, 128], BF16) for i in range(2)]
            raw = [T(stk, f"{tag}raw{i}", [128, n + 4], BF16) for i in range(2)]
            actb = [T(stk, f"{tag}act{i}", [128, n], BF16) for i in range(2)]
            xs = [T(stk, f"{tag}xs{i}", [128, nch, 128], BF16) for i in range(2)]
            dtmp, dtmpb = T(stk, f"{tag}dtmp", [128, 8, 64], F32)
            for i in range(2):
                P.op("dve", lambda e, i=i: e.memset(raw[i][0][:], 0.0), w=[raw[i][1]])
            P.dma("pool", wdt[:], w_in_r[:, :, C_DT:C_DT + 64], w=[wdtb])
            for t0 in range(0, nch, 8):
                m = min(8, nch - t0)
                ps, psb = PS[4]
                P.pe([(ps[:, 64 * j:64 * j + 64], [(hT[:, k, 2 + 128 * (t0 + j):2 + 128 * (t0 + j) + 128], wdt[:, k, :]) for k in range(8)])
                      for j in range(m)], r=[hTb, wdtb], w=[psb])
                P.op("dve", lambda e, m=m, ps=ps: e.tensor_tensor(out=dtmp[:, 0:m, :], in0=ps[:, 0:64 * m].rearrange("p (a b) -> p a b", b=64),
                     in1=vtm[:, O_DTB:O_DTB + 64].unsqueeze(1).broadcast_to([128, m, 64]), op=ALU.add), r=[psb, vtmb], w=[dtmpb])
                P.op("act", lambda e, m=m: e.activation(out=dtmp[:, 0:m, :], in_=dtmp[:, 0:m, :], func=AF.Exp), r=[dtmpb], w=[dtmpb])
                P.op("act", lambda e, m=m, t0=t0: e.activation(out=dt_all[:, t0:t0 + m, :], in_=dtmp[:, 0:m, :], func=AF.Ln, bias=1.0), r=[dtmpb], w=[dt_allb])
            nck = 4 * nslab
            slabs = {}

            def st1(c):
                sidx, cc = c // 4, c % 4
                if cc == 0:
                    ws, wsb = wsl[sidx % 2]
                    P.dma("pool", ws[:], w_in_r[:, :, 512 * sidx:512 * sidx + 512], w=[wsb])
                    slabs[sidx] = (ws, wsb)
                ws, wsb = slabs[sidx]
                dg_, dgb = dg[c % 2]; rw, rwb = raw[c % 2]
                for j in range(5):
                    col = O_SCW + c * 5 + j
                    P.op("dve", lambda e, j=j, col=col: e.tensor_scalar_mul(out=dg_[:, j, :], in0=id32, scalar1=vfm[:, col:col + 1]),
                         r=[c32b, vfmb], w=[dgb])
                for i in range(ntile):
                    bk = (i + c) % 2
                    ps, psb = PS[bk]
                    P.pe([(ps[:, 0:TS], [(ws[:, k, 128 * cc:128 * cc + 128], hT[:, k, 2 + TS * i:2 + TS * i + TS]) for k in range(8)])],
                         r=[wsb, hTb], w=[psb])
                    if bk == 0:
                        P.op("act", lambda e, ps=ps, i=i: e.activation(out=rw[:, 2 + TS * i:2 + TS * i + TS], in_=ps[:, 0:TS], func=AF.Copy), r=[psb], w=[rwb])
                    else:
                        P.op("dve", lambda e, ps=ps, i=i: e.tensor_copy(out=rw[:, 2 + TS * i:2 + TS * i + TS], in_=ps[:, 0:TS]), r=[psb], w=[rwb])

            def st2(c):
                dg_, dgb = dg[c % 2]; rw, rwb = raw[c % 2]; ab, abb = actb[c % 2]
                for i in range(ntile):
                    ps, psb = PS[2 + (i + c) % 2]
                    P.pe([(ps[:, 0:TS], [(dg_[:, j, :], rw[:, TS * i + j:TS * i + j + TS]) for j in range(5)])], r=[dgb, rwb], w=[psb])
                    P.op("act", lambda e, ps=ps, i=i: e.activation(out=ab[:, TS * i:TS * i + TS], in_=ps[:, 0:TS], func=AF.Silu,
                         bias=vfm[:, O_SCB + c:O_SCB + c + 1]), r=[psb, vfmb], w=[abb])

            def st3(c):
                ab, abb = actb[c % 2]; xs_, xsb = xs[c % 2]
                if c < 24:
                    for t0 in range(0, nch, 8):
                        m = min(8, nch - t0)
                        pb, pbb = PB[(t0 // 8) % 2]
                        P.tr([(pb[:, 128 * j:128 * j + 128], ab[:, 128 * (t0 + j):128 * (t0 + j) + 128]) for j in range(m)], idbf, r=[abb, cbfb], w=[pbb])
                        P.op("dve", lambda e, pb=pb, m=m, t0=t0: e.tensor_copy(out=xs_[:, t0:t0 + m, :],
                             in_=pb[:, 0:128 * m].rearrange("p (a b) -> p a b", b=128)), r=[pbb], w=[xsb])
                    dst, cc0 = (XTM, c) if c < 16 else (BTM, c - 16)
                    g = c - 16
                    for t0 in range(0, nch, 8):
                        m = min(8, nch - t0)
                        P.dma("sp", dst.rearrange("(t p) c -> p t c", p=128)[:, t0:t0 + m, 128 * cc0:128 * cc0 + 128], xs_[:, t0:t0 + m, :], r=[xsb],
                              w=([] if dbufs is None else [dbufs[0 if c < 16 else 1]]), sb=xsb)
                    if c >= 16:
                        if BTF is not None:
                            P.dma("sp", BTF[128 * g:128 * g + 128, :], ab[:], r=[abb])
                else:
                    g = c - 24
                    P.dma("sp", CTF[128 * g:128 * g + 128, :], ab[:], r=[abb])

            for step in range(nck + 2):
                if step < nck:
                    st1(step)
                if 0 <= step - 1 < nck:
                    st2(step - 1)
                if 0 <= step - 2 < nck:
                    st3(step - 2)

        def ssd_pass(tag, n, dirn, mode, XTM, BTM, BTF, CTF, dt_all, dt_allb, stk, Y_d=None, dbufs=None):
            nch = n // 128; TS = min(512, n); cpt = TS // 128
            order = list(range(nch)) if dirn == 0 else list(range(nch - 1, -1, -1))
            Mqbf = Ubf if dirn == 0 else Lbf
            Msbf = cbf[:, K_SL:K_SL + 128] if dirn == 0 else cbf[:, K_SU:K_SU + 128]
            Mcbf = Ubf if dirn == 0 else Lbf
            Mkbf = Ubf if dirn == 0 else Lbf
            hs, hsbs = hst[dirn]; hb, hbbs = hbf[dirn]
            full = mode != "state"
            Xr = [T(stk, f"{tag}X{i}", [128, 32, 64], BF16) for i in range(2)]
            Br = [T(stk, f"{tag}B{i}", [128, 1024], BF16) for i in range(2)]
            adtr = [T(stk, f"{tag}adt{i}", [128, 32], F32) for i in range(2)]
            adhl = [T(stk, f"{tag}adhl{i}", [128, 2, 32], BF16) for i in range(2)]
            exr = [T(stk, f"{tag}ex{i}", [128, 96], F32) for i in range(2)]
            wgtr = [T(stk, f"{tag}wgt{i}", [128, 32], F32) for i in range(2)]
            xdtwr = [T(stk, f"{tag}xdtw{i}", [128, 32, 64], BF16) for i in range(2)]
            if full:
                BTr = [T(stk, f"{tag}BT{i}", [128, 8, TS], BF16) for i in range(2)]
                CTr = [T(stk, f"{tag}CT{i}", [128, 8, TS], BF16) for i in range(2)]
                ah3r = [T(stk, f"{tag}ah3{i}", [128, 2, 96], BF16) for i in range(2)]
                c3r = [T(stk, f"{tag}c3{i}", [128, 128], BF16) for i in range(2)]
                nc3r = [T(stk, f"{tag}nc3{i}", [128, 128], BF16) for i in range(2)]
                R1, R1b = T(stk, f"{tag}R1", [128, 128], F32)
                tM, tMb = T(stk, f"{tag}tM", [128, 128], BF16)
                NEGbf = cbf[:, K_NEGF:K_NEGF + 128] if dirn == 0 else cbf[:, K_NEGB:K_NEGB + 128]
                xdtr = [T(stk, f"{tag}xdt{i}", [128, 32, 64], BF16) for i in range(2)]
                LT = [T(stk, f"{tag}LT{i}", [128, 4, 128], BF16) for i in range(2)]
                MT = [T(stk, f"{tag}MT{i}", [128, 4, 128], BF16) for i in range(2)]
                t1 = [T(stk, f"{tag}t1{i}", [128, 4, 64], BF16) for i in range(3)]
                ychr = [T(stk, f"{tag}ych{i}", [128, 2048], F32) for i in range(2)]
            state = {"cur_tile": None, "ntl": 0, "tiles": {}}
            ctx_ = {}
            ppend = {}

            def prologue(idx):
                t = order[idx]
                i = t // cpt
                c = {}
                if full:
                    if i != state["cur_tile"]:
                        state["cur_tile"] = i
                        BT, BTb = BTr[state["ntl"] % 2]; CT, CTb = CTr[state["ntl"] % 2]
                        state["ntl"] += 1
                        P.dma("act", BT[:], BTF.rearrange("(g n) t -> n g t", n=128)[:, :, TS * i:TS * i + TS], w=[BTb])
                        P.dma("act", CT[:], CTF.rearrange("(g n) t -> n g t", n=128)[:, :, TS * i:TS * i + TS], w=[CTb])
                        state["tiles"][i] = (BT, BTb, CT, CTb)
                    c["BT"], c["BTb"], c["CT"], c["CTb"] = state["tiles"][i]
                X, Xb = Xr[idx % 2]; Bt, Btb = Br[idx % 2]
                P.dma("sp", X[:], XTM[128 * t:128 * t + 128, :].rearrange("p (h d) -> p h d", d=64), w=[Xb], r=([] if dbufs is None else [dbufs[0]]))
                P.dma("sp", Bt[:], BTM[128 * t:128 * t + 128, :], w=[Btb], r=([] if dbufs is None else [dbufs[1]]))
                adt, adtb = adtr[idx % 2]; ah, ahb = adhl[idx % 2]; ex, exb = exr[idx % 2]; wgt, wgtb = wgtr[idx % 2]
                xdtw, xdtwb = xdtwr[idx % 2]
                dtcur = dt_all[:, t, 32 * dirn:32 * dirn + 32]
                P.op("dve", lambda e: e.tensor_tensor(out=adt[:], in0=dtcur, in1=abc[:, 32 * dirn:32 * dirn + 32], op=ALU.mult), r=[dt_allb, abcb], w=[adtb])
                P.op("dve", lambda e: e.tensor_copy(out=ah[:, 0, :], in_=adt[:]), r=[adtb], w=[ahb])
                P.op("dve", lambda e: e.tensor_tensor(out=ah[:, 1, :], in0=adt[:], in1=ah[:, 0, :], op=ALU.subtract), r=[adtb, ahb], w=[ahb])
                if full:
                    ah3, ah3b = ah3r[idx % 2]
                    P.op("dve", lambda e: e.tensor_copy(out=ah3[:].rearrange("p s (i r) -> p s i r", i=3), in_=ah[:].unsqueeze(2).broadcast_to([128, 2, 3, 32])),
                         r=[ahb], w=[ah3b])
                yield
                ps0, ps0b = PS[0]
                grp0 = [(ps0[:, 0:32], [(Mcbf, ah[:, 0, :]), (Mcbf, ah[:, 1, :])]), (ps0[:, 32:64], [(Msbf, ah[:, 0, :]), (Msbf, ah[:, 1, :])]),
                        (ps0[:, 64:96], [(onebf, ah[:, 0, :]), (onebf, ah[:, 1, :])])]
                rd0 = [cbfb, ahb]
                if full:
                    grp0.append((ps0[0:96, 128:256], [(ah3[:, 0, :], Mcbf), (ah3[:, 1, :], Mcbf)]))
                    rd0.append(ah3b)
                P.pe(grp0, r=rd0, w=[ps0b])
                if full:
                    c3, c3b = c3r[idx % 2]; nc3, nc3b = nc3r[idx % 2]
                    cps = ps0[0:96, 128:256]
                    P.op("dve", lambda e: e.tensor_copy(out=c3[0:96, :], in_=cps), r=[ps0b], w=[c3b])
                    P.op("dve", lambda e: e.tensor_tensor(out=R1[0:96, :], in0=cps, in1=c3[0:96, :], op=ALU.subtract), r=[ps0b, c3b], w=[R1b])
                    P.op("dve", lambda e: e.tensor_copy(out=c3[32:64, :], in_=R1[32:64, :]), r=[R1b], w=[c3b])
                    P.op("dve", lambda e: e.tensor_copy(out=tM[64:96, :], in_=R1[64:96, :]), r=[R1b], w=[tMb])
                    P.op("dve", lambda e: e.tensor_tensor(out=c3[64:96, :], in0=R1[64:96, :], in1=tM[64:96, :], op=ALU.subtract), r=[R1b, tMb], w=[c3b])
                    P.op("dve", lambda e: e.tensor_scalar_mul(out=nc3[0:96, :], in0=c3[0:96, :], scalar1=-1.0), r=[c3b], w=[nc3b])
                P.op("act", lambda e: e.activation(out=ex[:], in_=ps0[:, 0:96], func=AF.Exp), r=[ps0b], w=[exb])
                P.op("dve", lambda e: e.tensor_tensor(out=wgt[:], in0=ex[:, 32:64], in1=dtcur, op=ALU.mult), r=[exb, dt_allb], w=[wgtb])
                P.op("pool" if full else "dve", lambda e: e.tensor_tensor(out=xdtw[:], in0=X[:], in1=wgt[:].unsqueeze(2).broadcast_to([128, 32, 64]), op=ALU.mult), r=[Xb, wgtb], w=[xdtwb])
                c.update(t=t, off=(t % cpt) * 128, X=X, Xb=Xb, Bt=Bt, Btb=Btb, ex=ex, exb=exb, xdtw=xdtw, xdtwb=xdtwb)
                if full:
                    xdt, xdtb = xdtr[idx % 2]
                    c.update(c3=c3, c3b=c3b, nc3=nc3, nc3b=nc3b)
                    P.op("pool", lambda e: e.tensor_tensor(out=xdt[:], in0=X[:], in1=dtcur.unsqueeze(2).broadcast_to([128, 32, 64]), op=ALU.mult), r=[Xb, dt_allb], w=[xdtb])
                    c.update(xdt=xdt, xdtb=xdtb)
                ctx_[idx] = c

            def decay_state(idx):
                ex, exb = ctx_[idx]["ex"], ctx_[idx]["exb"]
                P.op("pool" if full else "dve", lambda e: e.tensor_tensor(out=hs[:].rearrange("p (a b) -> p a b", b=64), in0=hs[:].rearrange("p (a b) -> p a b", b=64),
                     in1=ex[:, 64:96].unsqueeze(2).broadcast_to([128, 32, 64]), op=ALU.mult), r=[exb] + hsbs, w=hsbs)

            cbanks = [PS[1], (PB[1][0][:].bitcast(F32), PB[1][1])]

            def pre(c, g):
                BT, BTb, CT, CTb, off = c["BT"], c["BTb"], c["CT"], c["CTb"], c["off"]
                c3, c3b, nc3, nc3b = c["c3"], c["c3b"], c["nc3"], c["nc3b"]
                cbk, cbkb = cbanks[g % 2]
                P.pe([(cbk[:, 0:128], [(BT[:, g, off:off + 128], CT[:, g, off:off + 128])])], r=[BTb, CTb], w=[cbkb])
                sg, sgb = PS[2 + g % 2]
                mms = [(sg[:], idbf, NEGbf.unsqueeze(1).broadcast_to([128, 4, 128]), True, False)]
                for r in range(4):
                    mms.append((sg[:, 128 * r:128 * r + 128], SEL[0:96, 4 * g + r, :], c3[0:96, :], False, False))
                mms.append((sg[:], nc3[0:96, :], SEL[0:96, 4 * g:4 * g + 4, :].rearrange("p a b -> p (a b)"), False, True))
                P.pe_raw(mms, r=[cbfb, SELb, c3b, nc3b], w=[sgb])

            def mid(g):
                cbk, cbkb = cbanks[g % 2]
                sg, sgb = PS[2 + g % 2]
                lt, ltb = LT[g % 2]; mt, mtb = MT[g % 2]
                P.op("act", lambda e: e.activation(out=lt[:].rearrange("p a b -> p (a b)"), in_=sg[:], func=AF.Exp), r=[sgb], w=[ltb])
                P.op("dve", lambda e: e.tensor_tensor(out=mt[:], in0=lt[:], in1=cbk[:, 0:128].unsqueeze(1).broadcast_to([128, 4, 128]), op=ALU.mult), r=[ltb, cbkb], w=[mtb])

            def groups(idx):
                c = ctx_.pop(idx)
                t, off, ex, exb = c["t"], c["off"], c["ex"], c["exb"]
                Bt, Btb, xdtw, xdtwb = c["Bt"], c["Btb"], c["xdtw"], c["xdtwb"]
                Sb = PS[0][1]
                Sfull = PS[0][0]
                ybanks = [PS[4], PS[5], (PB[0][0][:].bitcast(F32), PB[0][1])]
                if full:
                    CT, CTb = c["CT"], c["CTb"]
                    xdt, xdtb = c["xdt"], c["xdtb"]
                    ych, ychb = ychr[idx % 2]
                    if idx == 0:
                        pre(c, 0)
                        mid(0)
                def fin(g):
                    Yf, Yfb = ybanks[g % 3]; tf, tfb = t1[g % 3]
                    P.pe_raw([(Yf[:, 0:256], idbf, tf[:].rearrange("p a b -> p (a b)"), False, True)], r=[cbfb, tfb], w=[Yfb])
                    P.op("act", lambda e: e.activation(out=ych[:, 256 * g:256 * g + 256], in_=Yf[:, 0:256], func=AF.Copy), r=[Yfb], w=[ychb])

                def hbcopy(g):
                    P.op("act", lambda e: e.activation(out=hb[:, 256 * g:256 * g + 256], in_=hs[:, 256 * g:256 * g + 256], func=AF.Copy), r=[hsbs[g]], w=[hbbs[g]])
                for g in range(8):
                    so = 256
                    if full:
                        mt, mtb = MT[g % 2]
                        if g < 7:
                            pre(c, g + 1)
                            mid(g + 1)
                        elif idx + 1 < nch:
                            pre(ctx_[idx + 1], 0)
                            mid(0)
                        Y, Yb = ybanks[g % 3]
                        grp = [(Y[:, 64 * r:64 * r + 64], mt[:, r, :], xdt[:, 4 * g + r, :], r == 0, False) for r in range(4)]
                        grp.append((Y[:, 256:512], CT[:, g, off:off + 128], hb[:, 256 * g:256 * g + 256], False, False))
                        P.pe_raw(grp, r=[mtb, xdtb, CTb, hbbs[g]], w=[Yb])
                        if g >= 1:
                            fin(g - 1)
                    if g >= 1:
                        hbcopy(g - 1)
                    P.pe([(Sfull[:, so:so + 256], [(Bt[:, 128 * g:128 * g + 128], xdtw[:, 4 * g:4 * g + 4, :].rearrange("p a b -> p (a b)"))])], r=[Btb, xdtwb], w=[Sb])
                    if full:
                        t1_, t1b = t1[g % 3]
                        P.op("dve", lambda e: e.tensor_tensor(out=t1_[:], in0=Y[:, 256:512].rearrange("p (a b) -> p a b", b=64),
                             in1=ex[:, 4 * g:4 * g + 4].unsqueeze(2).broadcast_to([128, 4, 64]), op=ALU.mult), r=[Yb, exb], w=[t1b])
                    hv = hs[:, 256 * g:256 * g + 256]
                    P.op("dve", lambda e: e.tensor_tensor(out=hv, in0=hv, in1=Sfull[:, 256:512], op=ALU.add), r=[Sb, hsbs[g]], w=[hsbs[g]])
                    if g == 1 and "g" in ppend:
                        next(ppend.pop("g"), None)
                hbcopy(7)
                if full:
                    fin(7)
                    P.dma("sp", Y_d[128 * t:128 * t + 128, :], ych[:], r=[ychb])

            g0 = prologue(0)
            next(g0); next(g0, None)
            for idx in range(nch):
                decay_state(idx)
                if idx + 1 < nch:
                    ppend["g"] = prologue(idx + 1)
                    next(ppend["g"])
                groups(idx)
                if "g" in ppend:
                    next(ppend.pop("g"), None)

        def ssd_epilogue(stk):
            n = NT; nch = n // 128; TS = 512; cpt = 4
            wz, wzb = T(stk, "ewz", [128, 8, 2048], BF16)
            wzbs = [P.buf(f"wzp{q}") for q in range(4)]
            hTr = [T(stk, f"ehT{i}", [128, 8, TS], BF16) for i in range(2)]
            yfr = [T(stk, f"eyf{i}", [128, 2048], F32) for i in range(2)]
            ybr = [T(stk, f"eyb{i}", [128, 2048], F32) for i in range(2)]
            Xr = [T(stk, f"eX{i}", [128, 32, 64], BF16) for i in range(2)]
            xd, xdb = T(stk, "exd", [128, 32, 64], F32)
            zsr = [T(stk, f"ezs{i}", [128, 2048], F32) for i in range(2)]
            yznr = [T(stk, f"eyzn{i}", [128, 2048], BF16) for i in range(2)]
            ss2, ss2b = T(stk, "ess2", [128, 32], F32)
            yzT = [T(stk, f"eyzT{i}", [128, 16, TS], BF16) for i in range(2)]
            for q in range(4):
                P.dma("pool", wz[:, :, 512 * q:512 * q + 512], w_in_r[:, :, C_Z + 512 * q:C_Z + 512 * q + 512], w=[wzbs[q]])
            P.op("dve", lambda e: e.memset(ss2[:], 0.0), w=[ss2b])

            def load(t):
                i = t // cpt
                if t % cpt == 0:
                    hTt, hTtb = hTr[i % 2]
                    P.dma("sp", hTt[:], HT_d[:, :, 2 + TS * i:2 + TS * i + TS], w=[hTtb])
                yf, yfb = yfr[t % 2]; yb, ybb = ybr[t % 2]; X, Xb = Xr[t % 2]
                P.dma("sp", yf[:], YF_d[128 * t:128 * t + 128, :], w=[yfb])
                P.dma("act", yb[:], YB_d[128 * t:128 * t + 128, :], w=[ybb])
                P.dma("sp", X[:], XTM_d[128 * t:128 * t + 128, :].rearrange("p (h d) -> p h d", d=64), w=[Xb])
            def s1a(t):
                i = t // cpt; off = (t % cpt) * 128
                hTt, hTtb = hTr[i % 2]
                yf, yfb = yfr[t % 2]; yb, ybb = ybr[t % 2]; X, Xb = Xr[t % 2]
                zs, zsb = zsr[t % 2]
                P.op("pool", lambda e: e.tensor_tensor(out=yf[:], in0=yf[:], in1=yb[:], op=ALU.add), r=[ybb, yfb], w=[yfb])
                P.op("dve", lambda e: e.tensor_tensor(out=xd[:], in0=X[:], in1=vtm[:, O_SD:O_SD + 32].unsqueeze(2).broadcast_to([128, 32, 64]), op=ALU.mult),
                     r=[Xb, vtmb], w=[xdb])
                P.op("dve", lambda e: e.tensor_tensor(out=yf[:], in0=yf[:], in1=xd[:].rearrange("p a b -> p (a b)"), op=ALU.add), r=[xdb, yfb], w=[yfb])
                for q in range(4):
                    zp, zpb = PS[q % 4]
                    P.pe([(zp[:], [(hTt[:, k, off:off + 128], wz[:, k, 512 * q:512 * q + 512]) for k in range(8)])], r=[hTtb, wzbs[q]], w=[zpb])
                    P.op("act", lambda e, q=q, zp=zp: e.activation(out=zs[:, 512 * q:512 * q + 512], in_=zp[:], func=AF.Silu), r=[zpb], w=[zsb])

            def s1b(t):
                yf, yfb = yfr[t % 2]; zs, zsb = zsr[t % 2]; yzn, yznb = yznr[t % 2]
                P.op("dve", lambda e: e.tensor_tensor(out=zs[:], in0=yf[:], in1=zs[:], op=ALU.mult), r=[yfb, zsb], w=[zsb])
                sst = ss2[:, t:t + 1]
                P.op("act", lambda e: e.activation(out=yzn[:], in_=zs[:], func=AF.Square, accum_out=sst), r=[zsb], w=[yznb, ss2b])
                rstd_of(sst, ss2b, 2048)
                P.op("act", lambda e: e.activation(out=yzn[:], in_=zs[:], func=AF.Copy, scale=sst), r=[zsb, ss2b], w=[yznb])

            def s2(t):
                i = t // cpt; off = (t % cpt) * 128
                yzT_, yzTb = yzT[i % 2]; yzn, yznb = yznr[t % 2]
                for rd in range(2):
                    zt, ztb = PB[rd]
                    P.tr([(zt[:, 128 * j:128 * j + 128], yzn[:, 128 * (8 * rd + j):128 * (8 * rd + j) + 128]) for j in range(8)], idbf, r=[yznb, cbfb], w=[ztb])
                    P.op("dve", lambda e, rd=rd, zt=zt: e.tensor_copy(out=yzT_[:, 8 * rd:8 * rd + 8, off:off + 128],
                         in_=zt[:, 0:1024].rearrange("p (a b) -> p a b", b=128)), r=[ztb], w=[yzTb])
                if t % cpt == cpt - 1:
                    P.dma("sp", YZT_d.rearrange("(c f) t -> f c t", f=128)[:, :, TS * i:TS * i + TS], yzT_[:], r=[yzTb])

            load(0)
            load(1)
            s1a(0)
            s1b(0)
            for t in range(nch):
                if t + 2 < nch:
                    load(t + 2)
                if t + 1 < nch:
                    s1a(t + 1)
                s2(t)
                if t + 1 < nch:
                    s1b(t + 1)

        if upto >= 1:
            with ExitStack() as s1:
                hTc, hTcb = T(s1, "hTc", [128, 8, NCX + 4], BF16)
                cdb = [P.buf("XTMc_dram"), P.buf("BTMc_dram")]
                phase0("c0", ctx_d, NCX, lambda k: A1[:, k, 1:2], lambda k: modfm[:, k, 1:2], [A1b, modfmb], hTc, hTcb, s1)
                xbc_stage("c1", NCX, hTc, hTcb, s1, XTMc_d, BTMc_d, None, None, dtc, dtcb, 6, dbufs=cdb)
                for dirn in range(2):
                    ssd_pass(f"c2{dirn}", NCX, dirn, "state", XTMc_d, BTMc_d, None, None, dtc, dtcb, s1, dbufs=cdb)
                P.barrier()
            if debug:
                P.dma("sp", DBG_d[:, 2112:2112 + 2048], hst[0][0][:], r=hst[0][1])
                P.dma("sp", DBG_d[:, 4160:4160 + 2048], hst[1][0][:], r=hst[1][1])
                P.dma("sp", DBG_d[:, 6208:6208 + 128], dtc[:].rearrange("p a b -> p (a b)"), r=[dtcb])
        if upto >= 2:
            with ExitStack() as s2:
                hT, hTb = T(s2, "hT", [128, 8, NT + 4], BF16)
                with ExitStack() as s2a:
                    phase0("l0", x_d, NT, lambda k: A1[:, k, 0:1], lambda k: modfm[:, k, 0:1], [A1b, modfmb], hT, hTb, s2a)
                    P.dma("sp", HT_d, hT[:], r=[hTb])
                    P.barrier()
                with ExitStack() as s2b:
                    xbc_stage("l1", NT, hT, hTb, s2b, XTM_d, BTM_d, BTF_d, CTF_d, dtl, dtlb, 8)
                    P.barrier()
            if debug:
                P.dma("sp", DBG_d[:, 6336:6336 + 1024], dtl[:, 0:16, :].rearrange("p a b -> p (a b)"), r=[dtlb])
        if upto >= 3:
            with ExitStack() as s3:
                ssd_pass("f", NT, 0, "y", XTM_d, BTM_d, BTF_d, CTF_d, dtl, dtlb, s3, Y_d=YF_d)
                P.barrier()
        if upto >= 4:
            with ExitStack() as s4:
                ssd_pass("b", NT, 1, "y", XTM_d, BTM_d, BTF_d, CTF_d, dtl, dtlb, s4, Y_d=YB_d)
                P.barrier()

        gssd.close()
        if upto >= 4:
            with ExitStack() as s4e:
                dgt = [T(s4e, f"dgbuild{i}", [128, 31, 128], BF16) for i in range(2)]
                for ch in range(8):
                    dgc_, dgcb_ = dgt[ch % 2]
                    P.op("dve", lambda e, ch=ch, dgc_=dgc_: e.scalar_tensor_tensor(out=dgc_[:], in0=id32.unsqueeze(1).broadcast_to([128, 31, 128]), scalar=0.5,
                         in1=vfm[:, O_CDW + 31 * ch:O_CDW + 31 * ch + 31].unsqueeze(2).broadcast_to([128, 31, 128]), op0=ALU.mult, op1=ALU.mult), r=[c32b, vfmb], w=[dgcb_])
                    P.dma("act", DGC_d[ch].rearrange("p (a b) -> p a b", b=128), dgc_[:], r=[dgcb_])
                ssd_epilogue(s4e)
                P.barrier()
        if upto >= 5:
            with ExitStack() as s5:
                wgl, wglb = T(s5, "wgl", [128, 8, 2048], BF16)
                wco, wcob = T(s5, "wco", [128, 8, 1024], BF16)
                dgr = [T(s5, f"dgc{i}", [128, 31, 128], BF16) for i in range(3)]
                dgdb = [P.buf(f"dgc_dram{c}") for c in range(8)]
                hTr = [T(s5, f"cvhT{i}", [128, 8, 512], BF16) for i in range(2)]
                upad = [T(s5, f"upad{i}", [128, 8, 8, 94], BF16) for i in range(2)]
                sgr = [T(s5, f"cvsg{i}", [128, 512], F32) for i in range(2)]
                v32t = s5.enter_context(nc.sbuf_tensor("sb_v32", [128, 8, 512], F32)); v32bs = [P.buf(f"v32_{c}") for c in range(8)]
                vbft = s5.enter_context(nc.sbuf_tensor("sb_vbf", [128, 8, 512], BF16)); vbfbs = [P.buf(f"vbf_{c}") for c in range(8)]
                vsqt = s5.enter_context(nc.sbuf_tensor("sb_vsq", [128, 8, 512], BF16)); vsqbs = [P.buf(f"vsq_{c}") for c in range(8)]
                mean, meanb = T(s5, "mean", [128, 512], F32)
                var, varb = T(s5, "var", [128, 512], F32)
                tmpr = [T(s5, f"cvtmp{i}", [128, 512], F32) for i in range(2)]
                aat = s5.enter_context(nc.sbuf_tensor("sb_cvaa", [128, 8, 512], BF16)); aabs = [P.buf(f"aa_{c}") for c in range(8)]
                ucr = [T(s5, f"uc{i}", [128, 8, 512], BF16) for i in range(2)]
                wglbs = [P.buf(f"wglp{q}") for q in range(4)]
                wcobs = [P.buf(f"wcop{q}") for q in range(2)]
                for q in (0, 2, 1, 3):
                    P.dma("pool", wgl[:, :, 512 * q:512 * q + 512], w_in_r[:, :, C_GLU + 512 * q:C_GLU + 512 * q + 512], w=[wglbs[q]])
                for q in range(2):
                    P.dma("pool", wco[:, :, 512 * q:512 * q + 512], w_co_d.rearrange("(k p) c -> p k c", p=128)[:, :, 512 * q:512 * q + 512], w=[wcobs[q]])
                for i in range(2):
                    P.op("dve", lambda e, i=i: e.memset(upad[i][0][:], 0.0), w=[upad[i][1]])
                ntile = NT // 512
                cnt = {"dg": 0, "dgb": 0}

                def glu_mm(i, ch):
                    hTt, hTtb = hTr[i % 2]
                    p0, p0b = PS[ch % 2]; p1, p1b = PS[2 + ch % 2]
                    P.pe([(p0[:], [(wgl[:, k, 128 * ch:128 * ch + 128], hTt[:, k, :]) for k in range(8)])], r=[wglbs[ch // 4], hTtb], w=[p0b])
                    P.pe([(p1[:], [(wgl[:, k, 1024 + 128 * ch:1024 + 128 * ch + 128], hTt[:, k, :]) for k in range(8)])], r=[wglbs[2 + ch // 4], hTtb], w=[p1b])

                def build_dg(ch):
                    dgc, dgcb = dgr[cnt["dgb"] % 3]; cnt["dgb"] += 1
                    P.dma("sp", dgc[:], DGC_d[ch].rearrange("p (a b) -> p a b", b=128), w=[dgcb])

                pend = {}

                def headA(i, ch):
                    up, upb = upad[i % 2]
                    p0, p0b = PS[ch % 2]; p1, p1b = PS[2 + ch % 2]
                    sg, sgb = sgr[ch % 2]
                    dgc, dgcb = dgr[cnt["dg"] % 3]; cnt["dg"] += 1
                    P.op("act", lambda e: e.activation(out=sg[:], in_=p1[:], func=AF.Tanh, scale=0.5), r=[p1b], w=[sgb])
                    P.op("dve", lambda e: e.scalar_tensor_tensor(out=up[:, ch, :, 15:79], in0=sg[:].rearrange("p (a b) -> p a b", b=64), scalar=1.0,
                         in1=p0[:].rearrange("p (a b) -> p a b", b=64), op0=ALU.add, op1=ALU.mult), r=[p0b, sgb], w=[upb])
                    if ch < 7:
                        glu_mm(i, ch + 1)
                    build_dg((ch + 2) % 8)
                    pc, pcb = PS[4 + ch % 2]
                    P.pe([(pc[:], [(dgc[:, j, :], up[:, ch, :, j:j + 64]) for j in range(31)])], r=[dgcb, upb], w=[pcb])

                def headB(i, ch):
                    pc, pcb = PS[4 + ch % 2]
                    bcol = vfm[:, O_CDB + ch:O_CDB + ch + 1]
                    P.op("act", lambda e: e.activation(out=v32t[:, ch, :], in_=pc[:], func=AF.Identity, bias=bcol), r=[pcb, vfmb], w=[v32bs[ch]])
                    P.op("act", lambda e: e.activation(out=vsqt[:, ch, :], in_=pc[:], func=AF.Square, bias=bcol), r=[pcb, vfmb], w=[vsqbs[ch]])
                    P.op("dve", lambda e: e.tensor_copy(out=vbft[:, ch, :], in_=v32t[:, ch, :]), r=[v32bs[ch]], w=[vbfbs[ch]])

                def load(i):
                    hTt, hTtb = hTr[i % 2]
                    P.dma("sp", hTt[:], HT_d[:, :, 2 + 512 * i:2 + 512 * i + 512], w=[hTtb])

                load(0)
                glu_mm(0, 0)
                build_dg(0)
                build_dg(1)
                headA(0, 0)
                for ch in range(8):
                    headB(0, ch)
                    if ch < 7:
                        headA(0, ch + 1)
                for i in range(ntile):
                    uc, ucb = ucr[i % 2]
                    if i + 1 < ntile:
                        load(i + 1)
                    s4b, s4bb = PB[0]; s5b_, s5bb = PB[1]
                    s4f = s4b[:].bitcast(F32); s5f = s5b_[:].bitcast(F32)
                    P.pe([(s4f, [(onebf, vbft[:, ch, :]) for ch in range(8)])], r=[cbfb] + vbfbs, w=[s4bb])
                    P.pe([(s5f, [(onebf, vsqt[:, ch, :]) for ch in range(8)])], r=[cbfb] + vsqbs, w=[s5bb])
                    if i + 1 < ntile:
                        glu_mm(i + 1, 0)
                        headA(i + 1, 0)
                    tmp, tmpb = tmpr[0]
                    P.op("dve", lambda e: e.tensor_scalar_mul(out=mean[:], in0=s4f, scalar1=1.0 / 1024), r=[s4bb], w=[meanb])
                    P.op("dve", lambda e: e.tensor_tensor(out=tmp[:], in0=mean[:], in1=mean[:], op=ALU.mult), r=[meanb], w=[tmpb])
                    P.op("dve", lambda e: e.scalar_tensor_tensor(out=var[:], in0=s5f, scalar=1.0 / 1024, in1=tmp[:], op0=ALU.mult, op1=ALU.subtract), r=[s5bb, tmpb], w=[varb])
                    P.op("act", lambda e: e.activation(out=var[:], in_=var[:], func=AF.Sqrt, bias=EPS), r=[varb], w=[varb])
                    P.op("dve", lambda e: e.reciprocal(out=var[:], in_=var[:]), r=[varb], w=[varb])
                    for ch in range(8):
                        tmp, tmpb = tmpr[ch % 2]
                        P.op("dve", lambda e: e.tensor_tensor(out=tmp[:], in0=v32t[:, ch, :], in1=mean[:], op=ALU.subtract), r=[v32bs[ch], meanb], w=[tmpb])
                        P.op("dve", lambda e: e.tensor_tensor(out=tmp[:], in0=tmp[:], in1=var[:], op=ALU.mult), r=[tmpb, varb], w=[tmpb])
                        P.op("act", lambda e: e.activation(out=aat[:, ch, :], in_=tmp[:], func=AF.Silu, bias=vfm[:, O_CLB + ch:O_CLB + ch + 1],
                             scale=vfm[:, O_CLW + ch:O_CLW + ch + 1]), r=[tmpb, vfmb], w=[aabs[ch]])
                        if i + 1 < ntile:
                            headB(i + 1, ch)
                            if ch < 7:
                                headA(i + 1, ch + 1)
                    for dc in range(8):
                        pq, pqb = PS[4 + dc % 2]
                        P.pe([(pq[:], [(wco[:, ch, 128 * dc:128 * dc + 128], aat[:, ch, :]) for ch in range(8)])], r=[wcobs[dc // 4]] + aabs, w=[pqb])
                        P.op("act", lambda e, dc=dc, pq=pq: e.activation(out=uc[:, dc, :], in_=pq[:], func=AF.Identity, bias=vfm[:, O_BCO + dc:O_BCO + dc + 1]),
                             r=[pqb, vfmb], w=[ucb])
                    P.dma("sp", UC_d.rearrange("(c p) t -> p c t", p=128)[:, :, 512 * i:512 * i + 512], uc[:], r=[ucb])
                P.barrier()

        if upto >= 6:
            with ExitStack() as s6:
                wg, wgb = T(s6, "wg", [128, 8, 2048], BF16)
                wso, wsob = T(s6, "wso", [128, 16, 1024], BF16)
                wo, wob = T(s6, "wo", [128, 8, 1024], BF16)
                wobs = [P.buf(f"wop{q}") for q in range(2)]
                hTr = [T(s6, f"mhT{i}", [128, 8, 512], BF16) for i in range(2)]
                yzr = [T(s6, f"myz{i}", [128, 16, 512], BF16) for i in range(2)]
                ucr = [T(s6, f"muc{i}", [128, 8, 512], BF16) for i in range(2)]
                gcr = [T(s6, f"gc{i}", [128, 512], BF16) for i in range(2)]
                gsr = [T(s6, f"gs{i}", [128, 512], BF16) for i in range(2)]
                m1r = [T(s6, f"m1{i}", [128, 512], F32) for i in range(1)]
                m2, m2b = T(s6, "m2", [128, 512], F32)
                mg, mgb = T(s6, "mg", [128, 8, 512], BF16)
                xr = [T(s6, f"mx{i}", [128, D], F32) for i in range(2)]
                x1r = [T(s6, f"mx1{i}", [128, D], F32) for i in range(2)]
                xn, xnb = T(s6, "mxn", [128, D], BF16)
                ss, ssb = T(s6, "mss", [128, 32], F32)
                h2r = [T(s6, f"mh2{i}", [128, 8, 512], BF16) for i in range(1)]
                wgbs = [P.buf(f"wgp{q}") for q in range(4)]
                wsobs = [P.buf(f"wsop{q}") for q in range(2)]
                def ld_wg(q):
                    P.dma("pool", wg[:, :, 512 * q:512 * q + 512], w_in_r[:, :, C_GATE + 512 * q:C_GATE + 512 * q + 512], w=[wgbs[q]])

                def ld_wso(q):
                    P.dma("pool", wso[:, :, 512 * q:512 * q + 512], w_so_d.rearrange("(k p) c -> p k c", p=128)[:, :, 512 * q:512 * q + 512], w=[wsobs[q]])

                def ld_wo(q):
                    P.dma("pool", wo[:, :, 512 * q:512 * q + 512], w_o_d.rearrange("(k p) c -> p k c", p=128)[:, :, 512 * q:512 * q + 512], w=[wobs[q]])
                ld_wg(0); ld_wg(2); ld_wso(0); ld_wg(1); ld_wg(3); ld_wso(1); ld_wo(0); ld_wo(1)
                for q in range(2):
                    P.op("dve", lambda e, q=q: e.tensor_tensor(out=wso[:, :, 512 * q:512 * q + 512], in0=wso[:, :, 512 * q:512 * q + 512],
                         in1=vfm[:, O_SNW:O_SNW + 16].unsqueeze(2).broadcast_to([128, 16, 512]), op=ALU.mult), r=[vfmb, wsobs[q]], w=[wsobs[q]])
                P.op("dve", lambda e: e.memset(ss[:], 0.0), w=[ssb])
                ntile = NT // 512

                def load(i):
                    hTt, hTtb = hTr[i % 2]; yz, yzb = yzr[i % 2]; uc, ucb = ucr[i % 2]
                    P.dma("sp", hTt[:], HT_d[:, :, 2 + 512 * i:2 + 512 * i + 512], w=[hTtb])
                    P.dma("sp", yz[:], YZT_d.rearrange("(c f) t -> f c t", f=128)[:, :, 512 * i:512 * i + 512], w=[yzb])
                    P.dma("sp", uc[:], UC_d.rearrange("(c p) t -> p c t", p=128)[:, :, 512 * i:512 * i + 512], w=[ucb])
                load(0)
                s5pend = []
                for i in range(ntile):
                    hTt, hTtb = hTr[i % 2]; yz, yzb = yzr[i % 2]; uc, ucb = ucr[i % 2]; h2, h2b = h2r[0]
                    if i + 1 < ntile:
                        load(i + 1)
                    for dc in range(8):
                        pgc, pgcb = PS[0]; pgs, pgsb = PS[1]
                        gc_, gcb = gcr[dc % 2]; gs_, gsb = gsr[dc % 2]; m1, m1b = m1r[0]
                        P.pe([(pgc[:], [(wg[:, k, 128 * dc:128 * dc + 128], hTt[:, k, :]) for k in range(8)])], r=[wgbs[dc // 4], hTtb], w=[pgcb])
                        P.pe([(pgs[:], [(wg[:, k, 1024 + 128 * dc:1024 + 128 * dc + 128], hTt[:, k, :]) for k in range(8)])], r=[wgbs[2 + dc // 4], hTtb], w=[pgsb])
                        pu, pub = PS[2 + dc % 2]
                        P.pe([(pu[:], [(wso[:, c, 128 * dc:128 * dc + 128], yz[:, c, :]) for c in range(16)])], r=[wsobs[dc // 4], yzb], w=[pub])
                        if dc == 0 and s5pend:
                            s5pend.pop()()
                        P.op("act", lambda e: e.activation(out=gc_[:], in_=pgc[:], func=AF.Sigmoid), r=[pgcb], w=[gcb])
                        P.op("act", lambda e: e.activation(out=gs_[:], in_=pgs[:], func=AF.Sigmoid), r=[pgsb], w=[gsb])
                        P.op("dve", lambda e: e.tensor_tensor(out=m1[:], in0=pu[:], in1=gs_[:], op=ALU.mult), r=[pub, gsb], w=[m1b])
                        P.op("dve", lambda e: e.tensor_tensor(out=m2[:], in0=uc[:, dc, :], in1=gc_[:], op=ALU.mult), r=[ucb, gcb], w=[m2b])
                        P.op("dve", lambda e: e.tensor_tensor(out=mg[:, dc, :], in0=m1[:], in1=m2[:], op=ALU.add), r=[m1b, m2b], w=[mgb])
                    xnr = [(xn[:], xnb), (m2[:].bitcast(BF16), m2b)]

                    def partA(cq):
                        t = 4 * i + cq
                        x_, xb = xr[cq % 2]; x1, x1b = x1r[cq % 2]
                        xn_, xn_b = xnr[cq % 2]
                        P.dma("sp", x_[:], x_d[128 * t:128 * t + 128, :], w=[xb])
                        for half in range(2):
                            pm, pmb = PB[half]
                            pmf = pm[:].bitcast(F32)
                            P.pe([(pmf, [(mg[:, dc, 128 * cq:128 * cq + 128], wo[:, dc, 512 * half:512 * half + 512]) for dc in range(8)])], r=[mgb, wobs[half]], w=[pmb])
                            P.op("dve", lambda e: e.tensor_tensor(out=x1[:, 512 * half:512 * half + 512], in0=pmf, in1=g1bc[:, 512 * half:512 * half + 512], op=ALU.mult), r=[pmb, g1b], w=[x1b])
                            P.op("dve", lambda e: e.tensor_tensor(out=x1[:, 512 * half:512 * half + 512], in0=x1[:, 512 * half:512 * half + 512], in1=x_[:, 512 * half:512 * half + 512], op=ALU.add),
                                 r=[xb, x1b], w=[x1b])
                        P.dma("sp", X1_d[128 * t:128 * t + 128, :], x1[:], r=[x1b])
                        sst = ss[:, t:t + 1]
                        P.op("act", lambda e: e.activation(out=xn_, in_=x1[:], func=AF.Square, accum_out=sst), r=[x1b], w=[xn_b, ssb])
                        rstd_of(sst, ssb, D)
                        P.op("act", lambda e: e.activation(out=xn_, in_=x1[:], func=AF.Copy, scale=sst), r=[x1b, ssb], w=[xn_b])

                    def partB(cq):
                        xn_, xn_b = xnr[cq % 2]
                        pt, ptb = PS[4 + cq % 2]
                        ptv = pt[:].bitcast(BF16)
                        P.tr([(ptv[:, 128 * k:128 * k + 128], xn_[:, 128 * k:128 * k + 128]) for k in range(8)], idbf, r=[xn_b, cbfb], w=[ptb])
                        for k in range(8):
                            P.op("dve", lambda e, k=k: e.tensor_scalar(out=h2[:, k, 128 * cq:128 * cq + 128], in0=ptv[:, 128 * k:128 * k + 128],
                                 scalar1=A2[:, k, 0:1], scalar2=modfm[:, 16 + k, 0:1], op0=ALU.mult, op1=ALU.add), r=[ptb, A2b, modfmb], w=[h2b])

                    for cq in range(4):
                        partA(cq)
                        if cq >= 1:
                            partB(cq - 1)

                    def tail(i=i, partB=partB, h2=h2, h2b=h2b):
                        partB(3)
                        P.dma("sp", H2T_d[:, :, 512 * i:512 * i + 512], h2[:], r=[h2b])
                    s5pend.append(tail)
                s5pend.pop()()
                P.barrier()

        outbufs = []
        if upto >= 7:
            with ExitStack() as s7:
                w1, w1b = T(s7, "w1", [128, 8, 4096], BF16)
                w2, w2b = T(s7, "w2", [128, 32, 1024], BF16)
                w1bs = [P.buf(f"w1p{q}") for q in range(8)]
                w2bs = [[P.buf(f"w2p{q}{hh}") for hh in range(2)] for q in range(2)]
                h2r = [T(s7, f"fh2{i}", [128, 8, 512], BF16) for i in range(1)]
                hid, hidb = T(s7, "hid", [128, 32, 512], BF16)
                rl = [T(s7, f"rl{i}", [128, 512], F32) for i in range(1)]
                x1r = [T(s7, f"fx1{i}", [128, D], F32) for i in range(1)]
                x2, x2b = T(s7, "fx2", [128, D], F32)
                orr = [(x2, x2b)]
                ss, ssb = T(s7, "fss", [128, 32], F32)
                for q in range(8):
                    P.dma("pool", w1[:, :, 512 * q:512 * q + 512], w_m1_d.rearrange("(k p) c -> p k c", p=128)[:, :, 512 * q:512 * q + 512], w=[w1bs[q]])
                for q in range(2):
                    for hh in range(2):
                        P.dma("pool", w2[:, 16 * hh:16 * hh + 16, 512 * q:512 * q + 512],
                              w_m2_d.rearrange("(k p) c -> p k c", p=128)[:, 16 * hh:16 * hh + 16, 512 * q:512 * q + 512], w=[w2bs[q][hh]])
                P.op("dve", lambda e: e.memset(ss[:], 0.0), w=[ssb])
                h2, h2b = h2r[0]
                P.dma("sp", h2[:], H2T_d[:, :, 0:512], w=[h2b])
                for i in range(NT // 512):
                    for f in range(32):
                        pf, pfb = PS[f % 2]; r_, rb = rl[0]
                        P.pe([(pf[:], [(w1[:, k, 128 * f:128 * f + 128], h2[:, k, :]) for k in range(8)])], r=[w1bs[f // 4], h2b], w=[pfb])
                        P.op("act", lambda e, pf=pf, r_=r_: e.activation(out=r_[:], in_=pf[:], func=AF.Relu), r=[pfb], w=[rb])
                        P.op("dve", lambda e, f=f, r_=r_: e.tensor_tensor(out=hid[:, f, :], in0=r_[:], in1=r_[:], op=ALU.mult), r=[rb], w=[hidb])
                    if i + 1 < NT // 512:
                        P.dma("sp", h2[:], H2T_d[:, :, 512 * (i + 1):512 * (i + 1) + 512], w=[h2b])
                    for cq in range(4):
                        t = 4 * i + cq
                        x1, x1b = x1r[0]; o_, ob = orr[0]
                        P.dma("sp", x1[:], X1_d[128 * t:128 * t + 128, :], w=[x1b])
                        for half in range(2):
                            pm, pmb = PS[2 + half]
                            P.pe([(pm[:], [(hid[:, f, 128 * cq:128 * cq + 128], w2[:, f, 512 * half:512 * half + 512]) for f in range(32)])], r=[hidb] + w2bs[half], w=[pmb])
                            P.op("dve", lambda e, pm=pm, half=half: e.tensor_tensor(out=x2[:, 512 * half:512 * half + 512], in0=pm[:], in1=g5bc[:, 512 * half:512 * half + 512], op=ALU.mult), r=[pmb, g5b], w=[x2b])
                            P.op("dve", lambda e, half=half, x1=x1: e.tensor_tensor(out=x2[:, 512 * half:512 * half + 512], in0=x2[:, 512 * half:512 * half + 512], in1=x1[:, 512 * half:512 * half + 512], op=ALU.add),
                                 r=[x1b, x2b], w=[x2b])
                        sst = ss[:, t:t + 1]
                        jk, jkb = rl[0]
                        P.op("act", lambda e: e.activation(out=jk[:].bitcast(BF16), in_=x2[:], func=AF.Square, accum_out=sst), r=[x2b], w=[jkb, ssb])
                        rstd_of(sst, ssb, D)
                        P.op("dve", lambda e, o_=o_: e.scalar_tensor_tensor(out=o_[:], in0=x2[:], scalar=sst, in1=vtm[:, O_FNW:O_FNW + 1024], op0=ALU.mult, op1=ALU.mult),
                             r=[x2b, ssb, vtmb], w=[ob])
                        P.dma("sp", out_d[128 * t:128 * t + 128, :], o_[:], r=[ob])
                P.barrier()
        P.barrier()
    return nc


def _prep(inputs):
    f = lambda a: np.ascontiguousarray(np.asarray(a, dtype=np.float32))
    fm = lambda v: f(v).reshape(-1, 128).T
    c = f(inputs["c"]); cctx = f(inputs["c_ctx"])
    vfm = np.zeros((128, NV), np.float32)
    vfm[:, O_BADA:O_BADA + 48] = fm(inputs["b_ada"][0])
    vfm[:, O_N1:O_N1 + 8] = fm(inputs["norm1_w"][0]); vfm[:, O_N2:O_N2 + 8] = fm(inputs["norm2_w"][0])
    cdw = f(inputs["conv_dw_w"][0])
    vfm[:, O_CDW:O_CDW + 248] = cdw.reshape(31, 8, 128).transpose(2, 1, 0).reshape(128, 248)
    vfm[:, O_CDB:O_CDB + 8] = fm(inputs["conv_dw_b"][0]); vfm[:, O_CLW:O_CLW + 8] = fm(inputs["conv_ln_w"][0])
    vfm[:, O_CLB:O_CLB + 8] = fm(inputs["conv_ln_b"][0]); vfm[:, O_BCO:O_BCO + 8] = fm(inputs["b_conv_out"][0])
    scw = f(inputs["ssm_conv_w"][0])
    vfm[:, O_SCW:O_SCW + 160] = scw.reshape(5, 32, 128).transpose(2, 1, 0).reshape(128, 160)
    vfm[:, O_SCB:O_SCB + 32] = fm(inputs["ssm_conv_b"][0]); vfm[:, O_SNW:O_SNW + 16] = fm(inputs["ssm_norm_w"][0])
    row = np.zeros((NW,), np.float32)
    ba = f(inputs["b_ada"][0])
    rowa = np.concatenate([ba[2048:3072], ba[5120:6144]])
    vtma = np.ascontiguousarray(np.broadcast_to(rowa[None, :], (128, NWA)))
    row[O_FNW:O_FNW + 1024] = f(inputs["final_norm_w"])
    row[O_DTB:O_DTB + 64] = f(inputs["ssm_dt_bias"][0]).reshape(64); row[O_ALOG:O_ALOG + 64] = f(inputs["ssm_a_log"][0]).reshape(64)
    row[O_SD:O_SD + 32] = f(inputs["ssm_d"][0])
    vtm = np.ascontiguousarray(np.broadcast_to(row[None, :], (128, NW)))
    idx = np.arange(128)
    j, k = idx[:, None], idx[None, :]
    consts = np.concatenate([(j == k), (j <= k), (j >= k), (j > k), (j < k), np.ones((128, 128), bool)], axis=1).astype(np.float32)
    negf = np.where(k >= j, 0.0, -30000.0).astype(np.float32)
    negb = np.where(k <= j, 0.0, -30000.0).astype(np.float32)
    consts = np.concatenate([consts, negf, negb], axis=1)
    sel = np.zeros((128, 32, 128), np.float32)
    for p in range(96):
        sel[p, p % 32, :] = 1.0
    sel = sel.reshape(128, 4096)
    shared = dict(vfm=vfm, vtm=vtm, vtma=vtma, consts=consts, sel=sel, w_ada=f(inputs["w_ada"][0]), w_in=f(inputs["w_in"][0]), w_conv_out=f(inputs["w_conv_out"][0]),
                  w_ssm_out=f(inputs["w_ssm_out"][0]), w_o=f(inputs["w_o"][0]), w_mlp1=f(inputs["w_mlp1"][0]), w_mlp2=f(inputs["w_mlp2"][0]))
    maps = []
    x = inputs["x"]; ctx = inputs["ctx"]
    for b in range(x.shape[0]):
        cT = np.stack([c[b].reshape(8, 128).T, cctx.reshape(8, 128).T], axis=2).reshape(128, 16)
        m = dict(shared)
        m["x"] = f(x[b]); m["ctx"] = f(ctx[b]); m["cT"] = np.ascontiguousarray(cT)
        maps.append(m)
    return maps


def kernel(**inputs):
    maps = _prep(inputs)
    nc = build()
    res = run_bass_kernel_spmd(nc, maps, core_ids=list(range(len(maps))))
    return np.stack([np.asarray(r["out"], dtype=np.float32) for r in res.results], axis=0)
```
